# Optimizing a Trainium2 kernel written in Bass

```python
import math
import jax, jax.numpy as jnp
from jax import lax
import numpy as np

D_MODEL = 1024
BATCH = 8
SEQ = 2048
DEPTH = 1
DEC_BATCH = 32
DEC_SEQ = 4
PAST_LEN = 8192
PAGE_SIZE = 128

D_MIX = D_MODEL
D_ATTN = D_MIX // 2
D_CONV = D_MIX - D_ATTN
N_HEADS = 4
HEAD_DIM = D_ATTN // (2 * N_HEADS)
V_DIM = 2 * HEAD_DIM
CONV_WIDTH = 31
D_FF = 4 * D_MODEL
D_PLE = 256
Q_BLOCK = 128
EPS = 1e-6
N_IN = 3 * D_ATTN + 2 * D_CONV

kernel_name = 'hymba_conformer_diffattn_step'


def rms_norm(x, g):
    xf = x.astype(jnp.float32)
    y = xf * lax.rsqrt(jnp.mean(xf * xf, axis=-1, keepdims=True) + EPS)
    return (y * g.astype(jnp.float32)).astype(x.dtype)


def layer_norm(x, g, b):
    xf = x.astype(jnp.float32)
    mu = jnp.mean(xf, axis=-1, keepdims=True)
    var = jnp.mean(jnp.square(xf - mu), axis=-1, keepdims=True)
    y = (xf - mu) * lax.rsqrt(var + EPS) * g.astype(jnp.float32) + b.astype(jnp.float32)
    return y.astype(x.dtype)


def alibi_slopes():
    return jnp.asarray([2.0 ** (-8.0 * (h + 1) / N_HEADS) for h in range(N_HEADS)], dtype=jnp.float32)


def diff_lambda(lq1, lk1, lq2, lk2, lam_init):
    f = jnp.float32
    return (jnp.exp(jnp.sum(lq1.astype(f) * lk1.astype(f)))
            - jnp.exp(jnp.sum(lq2.astype(f) * lk2.astype(f))) + lam_init)


def diff_attend(q, k, v, q_pos, k_pos, lam):
    s = jnp.einsum('bqhcd,bkhcd->bhcqk', q, k).astype(jnp.float32) * (HEAD_DIM ** -0.5)
    dist = (q_pos[:, None] - k_pos[None, :]).astype(jnp.float32)
    s = s - alibi_slopes()[None, :, None, None, None] * dist
    s = jnp.where(k_pos[None, :] <= q_pos[:, None], s, -jnp.inf)
    pr = jax.nn.softmax(s, axis=-1)
    w = pr[:, :, 0] - lam * pr[:, :, 1]
    return jnp.einsum('bhqk,bkhe->bqhe', w.astype(v.dtype), v)


def causal_depthwise_conv(u_ext, w_dw, b_dw):
    y = lax.conv_general_dilated(u_ext, w_dw[:, None, :], window_strides=(1,), padding='VALID',
                                 dimension_numbers=('NWC', 'WIO', 'NWC'), feature_group_count=D_CONV)
    return y + b_dw


def mixer(xn, lp, lam_init, past_k, past_v, conv_prev):
    B, T, _ = xn.shape
    z = xn @ lp['w_in']
    q = z[..., :D_ATTN].reshape(B, T, N_HEADS, 2, HEAD_DIM)
    k = z[..., D_ATTN:2 * D_ATTN].reshape(B, T, N_HEADS, 2, HEAD_DIM)
    v = z[..., 2 * D_ATTN:3 * D_ATTN].reshape(B, T, N_HEADS, V_DIM)
    a, gt = jnp.split(z[..., 3 * D_ATTN:], 2, axis=-1)
    u = a * jax.nn.sigmoid(gt)
    lam = diff_lambda(lp['lambda_q1'], lp['lambda_k1'], lp['lambda_q2'], lp['lambda_k2'], lam_init)
    if past_k is None:
        nb = T // Q_BLOCK
        qb = jnp.moveaxis(q.reshape(B, nb, Q_BLOCK, N_HEADS, 2, HEAD_DIM), 1, 0)
        k_pos = jnp.arange(T)

        def block(args):
            q_blk, i = args
            return diff_attend(q_blk, k, v, i * Q_BLOCK + jnp.arange(Q_BLOCK), k_pos, lam)

        o = lax.map(block, (qb, jnp.arange(nb)))
        o_attn = jnp.moveaxis(o, 0, 1).reshape(B, T, N_HEADS, V_DIM)
        u_ext = jnp.pad(u, ((0, 0), (CONV_WIDTH - 1, 0), (0, 0)))
    else:
        P = past_k.shape[1]
        k_all = jnp.concatenate([past_k, k], axis=1)
        v_all = jnp.concatenate([past_v, v], axis=1)
        o_attn = diff_attend(q, k_all, v_all, P + jnp.arange(T), jnp.arange(P + T), lam)
        u_ext = jnp.concatenate([conv_prev, u], axis=1)
    conv_state = u_ext[:, -(CONV_WIDTH - 1):]
    y_attn = (rms_norm(o_attn, lp['g_subln']) * (1.0 - lam_init)).reshape(B, T, D_ATTN)
    c = causal_depthwise_conv(u_ext, lp['w_dw'], lp['b_dw'])
    y_conv = jax.nn.silu(layer_norm(c, lp['ln_conv_g'], lp['ln_conv_b']))
    out = jnp.concatenate([y_attn, y_conv], axis=-1) @ lp['w_out']
    return out, k, v, conv_state


def decoder_layer(x, pe, lp, lam_init, past_k, past_v, conv_prev):
    m, k, v, conv_state = mixer(rms_norm(x, lp['g_pre_mix']), lp, lam_init, past_k, past_v, conv_prev)
    h = x + rms_norm(m, lp['g_post_mix'])
    f = jnp.square(jax.nn.relu(rms_norm(h, lp['g_pre_ffn']) @ lp['w_ff1'])) @ lp['w_ff2']
    h = h + rms_norm(f, lp['g_post_ffn'])
    h = h + jax.nn.sigmoid(h @ lp['w_ple_gate']) * (pe @ lp['w_ple'])
    return h, k, v, conv_state


def setup_inputs(seed: int = 0) -> dict:
    key = jax.random.key(seed)
    ks = jax.random.split(key, 32)
    f = jnp.float32
    n_pages = PAST_LEN // PAGE_SIZE
    n_pool = (DEC_BATCH * n_pages * 5) // 4

    def nrm(k, shape, scale):
        return jax.random.normal(k, shape, f) * scale

    def gain(k, shape):
        return 1.0 + 0.02 * jax.random.normal(k, shape, f)

    perm = jax.random.permutation(ks[0], n_pool)[: DEC_BATCH * n_pages]
    page_table = perm.reshape(DEC_BATCH, n_pages).astype(jnp.int32)
    return {
        'x_prompt': nrm(ks[1], (BATCH, SEQ, D_MODEL), 1.0),
        'x_sample': nrm(ks[2], (DEC_BATCH, DEC_SEQ, D_MODEL), 1.0),
        'cache_k': nrm(ks[3], (DEPTH, n_pool, PAGE_SIZE, N_HEADS, 2, HEAD_DIM), 1.0),
        'cache_v': nrm(ks[4], (DEPTH, n_pool, PAGE_SIZE, N_HEADS, V_DIM), 1.0),
        'state_conv': nrm(ks[5], (DEPTH, DEC_BATCH, CONV_WIDTH - 1, D_CONV), 0.5),
        'page_table': page_table,
        'p_prompt': nrm(ks[6], (DEPTH, BATCH, SEQ, D_PLE), 1.0),
        'p_sample': nrm(ks[7], (DEPTH, DEC_BATCH, DEC_SEQ, D_PLE), 1.0),
        'w_in': nrm(ks[8], (DEPTH, D_MODEL, N_IN), D_MODEL ** -0.5),
        'w_out': nrm(ks[9], (DEPTH, D_MIX, D_MODEL), D_MIX ** -0.5),
        'lambda_q1': nrm(ks[10], (DEPTH, HEAD_DIM), 0.1),
        'lambda_k1': nrm(ks[11], (DEPTH, HEAD_DIM), 0.1),
        'lambda_q2': nrm(ks[12], (DEPTH, HEAD_DIM), 0.1),
        'lambda_k2': nrm(ks[13], (DEPTH, HEAD_DIM), 0.1),
        'g_subln': gain(ks[14], (DEPTH, V_DIM)),
        'w_dw': nrm(ks[15], (DEPTH, CONV_WIDTH, D_CONV), CONV_WIDTH ** -0.5),
        'b_dw': nrm(ks[16], (DEPTH, D_CONV), 0.02),
        'ln_conv_g': gain(ks[17], (DEPTH, D_CONV)),
        'ln_conv_b': nrm(ks[18], (DEPTH, D_CONV), 0.02),
        'g_pre_mix': gain(ks[19], (DEPTH, D_MODEL)),
        'g_post_mix': gain(ks[20], (DEPTH, D_MODEL)),
        'g_pre_ffn': gain(ks[21], (DEPTH, D_MODEL)),
        'g_post_ffn': gain(ks[22], (DEPTH, D_MODEL)),
        'w_ff1': nrm(ks[23], (DEPTH, D_MODEL, D_FF), D_MODEL ** -0.5),
        'w_ff2': nrm(ks[24], (DEPTH, D_FF, D_MODEL), D_FF ** -0.5),
        'w_ple': nrm(ks[25], (DEPTH, D_PLE, D_MODEL), D_PLE ** -0.5),
        'w_ple_gate': nrm(ks[26], (DEPTH, D_MODEL, D_MODEL), D_MODEL ** -0.5),
    }


def reference(x_prompt, x_sample, cache_k, cache_v, state_conv, page_table, p_prompt, p_sample,
              w_in, w_out, lambda_q1, lambda_k1, lambda_q2, lambda_k2, g_subln, w_dw, b_dw,
              ln_conv_g, ln_conv_b, g_pre_mix, g_post_mix, g_pre_ffn, g_post_ffn,
              w_ff1, w_ff2, w_ple, w_ple_gate):
    hp, hs = x_prompt, x_sample
    DB = x_sample.shape[0]
    kp_l, vp_l, cp_l, ks_l, vs_l, cs_l = [], [], [], [], [], []
    for l in range(DEPTH):
        lam_init = 0.8 - 0.6 * math.exp(-0.3 * l)
        lp = {
            'w_in': w_in[l], 'w_out': w_out[l],
            'lambda_q1': lambda_q1[l], 'lambda_k1': lambda_k1[l],
            'lambda_q2': lambda_q2[l], 'lambda_k2': lambda_k2[l],
            'g_subln': g_subln[l], 'w_dw': w_dw[l], 'b_dw': b_dw[l],
            'ln_conv_g': ln_conv_g[l], 'ln_conv_b': ln_conv_b[l],
            'g_pre_mix': g_pre_mix[l], 'g_post_mix': g_post_mix[l],
            'g_pre_ffn': g_pre_ffn[l], 'g_post_ffn': g_post_ffn[l],
            'w_ff1': w_ff1[l], 'w_ff2': w_ff2[l],
            'w_ple': w_ple[l], 'w_ple_gate': w_ple_gate[l],
        }
        hp, kp, vp, cp = decoder_layer(hp, p_prompt[l], lp, lam_init, None, None, None)
        past_k = cache_k[l][page_table].reshape(DB, -1, N_HEADS, 2, HEAD_DIM)
        past_v = cache_v[l][page_table].reshape(DB, -1, N_HEADS, V_DIM)
        hs, kn, vn, cn = decoder_layer(hs, p_sample[l], lp, lam_init, past_k, past_v, state_conv[l])
        kp_l.append(kp); vp_l.append(vp); cp_l.append(cp)
        ks_l.append(kn); vs_l.append(vn); cs_l.append(cn)
    k_prompt = jnp.stack(kp_l)
    v_prompt = jnp.stack(vp_l)
    conv_prompt = jnp.stack(cp_l)
    k_sample = jnp.stack(ks_l)
    v_sample = jnp.stack(vs_l)
    conv_sample = jnp.stack(cs_l)
    return (hp, hs, k_prompt, v_prompt, conv_prompt, k_sample, v_sample, conv_sample)
```

```python
import contextlib
import numpy as np
import concourse.bass as bass
import concourse.mybir as mybir
from concourse.bass_utils import run_bass_kernel_spmd

F32 = mybir.dt.float32
BF16 = mybir.dt.bfloat16
I32 = mybir.dt.int32
AF = mybir.ActivationFunctionType
ALU = mybir.AluOpType
AX = mybir.AxisListType

D = 1024
T = 2048
NT = 17
NC_ = 2176
NPG = 64
import os
NPOOL = int(os.environ.get('K_NPOOL', '2560'))
DFF = 4096
EPS = 1e-6
LAM_INIT = 0.2
SLOPES = [2.0 ** (-2.0 * (h + 1)) for h in range(4)]


class Ev:
    __slots__ = ("sem", "val", "eng", "slot")

    def __init__(self, sem, val, eng, slot=None):
        self.sem, self.val, self.eng, self.slot = sem, val, eng, slot

    def value(self):
        return self.slot.count if self.slot is not None else self.val


class Res:
    __slots__ = ("name", "w", "r", "const")

    def __init__(self, name="", const=False):
        self.name, self.w, self.r, self.const = name, None, {}, const


class Slot:
    def __init__(self, sem, group=False):
        self.sem, self.count, self.group = sem, 0, group


class Sched:
    ENG = ("pe", "act", "dve", "pool", "sp")

    def __init__(self, nc, sems):
        self.nc = nc
        self._sems = list(sems)
        self.streams = {e: [] for e in self.ENG}
        self.esem = {e: self._sems.pop() for e in ("pe", "act", "dve", "pool")}
        self.cnt = {e: 0 for e in self.ENG}
        self.waited = {e: {} for e in self.ENG}
        self.out_events = []
        self.slots = []
        self._excl = None

    def slot(self, group=False):
        s = Slot(self._sems.pop(), group)
        self.slots.append(s)
        return s

    def _need(self, eng, ev, waits, raw):
        if ev is None or (ev.slot is not None and ev.slot is self._excl):
            return
        if ev.eng == eng and (not raw or eng in ("pe", "sp")):
            return
        key = id(ev.sem)
        if ev.slot is not None:
            if self.waited[eng].get((key, "f")):
                return
            self.waited[eng][(key, "f")] = True
            waits.append(ev)
            return
        if ev.val <= self.waited[eng].get(key, -1):
            return
        self.waited[eng][key] = ev.val
        waits.append(ev)

    def _deps(self, eng, reads, writes):
        waits = []
        for r in reads:
            self._need(eng, r.w, waits, True)
        for w in writes:
            self._need(eng, w.w, waits, False)
            for ev in w.r.values():
                self._need(eng, ev, waits, False)
        return waits

    def _record(self, ev, key, reads, writes):
        for r in reads:
            if not r.const:
                r.r[key] = ev
        for w in writes:
            w.w = ev
            w.r = {}

    def op(self, eng, fn, reads=(), writes=()):
        waits = self._deps(eng, reads, writes)
        self.cnt[eng] += 1
        ev = Ev(self.esem[eng], self.cnt[eng], eng)
        self.streams[eng].append((waits, fn, (self.esem[eng], 1)))
        self._record(ev, eng, reads, writes)
        return ev

    def dma(self, q, fn, slot, reads=(), writes=(), is_output=False):
        self._excl = slot if slot.group else None
        waits = self._deps(q, reads, writes)
        self._excl = None
        slot.count += 16
        ev = Ev(slot.sem, None if slot.group else slot.count, "dma", slot if slot.group else None)
        self.streams[q].append((waits, fn, (slot.sem, 16)))
        self._record(ev, ("dma", id(slot.sem)), reads, writes)
        if is_output:
            self.out_events.append(ev)
        return ev

    def barrier(self, exclude=()):
        evs = [Ev(self.esem[e], self.cnt[e], e) for e in ("pe", "act", "dve", "pool") if self.cnt[e] > 0]
        devs = [Ev(s.sem, s.count, "dma") for s in self.slots if s.count > 0 and s not in exclude]
        for eng in self.ENG:
            waits = []
            for ev in evs + devs:
                if ev.eng == eng:
                    continue
                self._need(eng, ev, waits, False)
            if waits:
                self.streams[eng].append((waits, None, None))

    def emit(self):
        nc = self.nc
        fin = {}
        for ev in self.out_events:
            k, v = id(ev.sem), ev.value()
            if k not in fin or fin[k][1] < v:
                fin[k] = (ev.sem, v)
        streams = self.streams

        def replay(name, e):
            for waits, fn, inc in streams[name]:
                for ev in waits:
                    e.wait_ge(ev.sem, ev.value())
                if fn is None:
                    continue
                ins = fn(e)
                if inc is not None:
                    ins.then_inc(inc[0], inc[1])
            if name == "sp":
                for sem, v in fin.values():
                    e.wait_ge(sem, v)

        with nc.Block() as block:
            @block.tensor
            def _(e):
                replay("pe", e)

            @block.scalar
            def _(e):
                replay("act", e)

            @block.vector
            def _(e):
                replay("dve", e)

            @block.gpsimd
            def _(e):
                replay("pool", e)

            @block.sync
            def _(e):
                replay("sp", e)


def build_program(stop_after="all"):
    nc = bass.Bass("TRN2", target_bir_lowering=False)

    def din(name, shape, dt=F32):
        return nc.dram_tensor(name, list(shape), dt, kind="ExternalInput").ap()

    def dout(name, shape, dt=F32):
        return nc.dram_tensor(name, list(shape), dt, kind="ExternalOutput").ap()

    x_p = din("x_p", [T, D])
    x_s = din("x_s", [16, D])
    cache_k = din("cache_k", [NPOOL * 128, 512])
    cache_v = din("cache_v", [NPOOL * 128, 512])
    st_conv = din("st_conv", [4 * 30, 512])
    ptab = din("ptab", [1, 4 * NPG], I32)
    p_p = din("p_p", [T, 256])
    p_s = din("p_s", [16, 256])
    w_in = din("w_in", [D, 2560])
    w_out = din("w_out", [D, D])
    lamq1 = din("lamq1", [1, 64])
    lamk1 = din("lamk1", [1, 64])
    lamq2 = din("lamq2", [1, 64])
    lamk2 = din("lamk2", [1, 64])
    g_subln = din("g_subln", [1, 128])
    w_dw = din("w_dw", [31, 512])
    b_dw = din("b_dw", [1, 512])
    ln_g = din("ln_g", [1, 512])
    ln_b = din("ln_b", [1, 512])
    g_pre_mix = din("g_pre_mix", [1, D])
    g_post_mix = din("g_post_mix", [1, D])
    g_pre_ffn = din("g_pre_ffn", [1, D])
    g_post_ffn = din("g_post_ffn", [1, D])
    w_ff1 = din("w_ff1", [D, DFF])
    w_ff2 = din("w_ff2", [DFF, D])
    w_ple = din("w_ple", [256, D])
    w_gate = din("w_gate", [D, D])
    y_p = dout("y_p", [T, D])
    y_s = dout("y_s", [16, D])
    k_p = dout("k_p", [T, 512])
    v_p = dout("v_p", [T, 512])
    c_p = dout("c_p", [30, 512])
    k_s = dout("k_s", [16, 512])
    v_s = dout("v_s", [16, 512])
    c_s = dout("c_s", [4 * 30, 512])
    dbg_ym = dout("dbg_ym", [128, 8 * NC_], BF16) if stop_after != "all" else None
    dbg_f = dout("dbg_f", [128, 4096]) if stop_after != "all" else None
    sc_out = nc.dram_tensor("sc_out", [D, D], BF16).ap()
    sc_ff1 = nc.dram_tensor("sc_ff1", [D, DFF], BF16).ap()
    sc_ff2 = nc.dram_tensor("sc_ff2", [DFF, D], BF16).ap()
    sc_gate = nc.dram_tensor("sc_gate", [D, D], BF16).ap()
    sc_ple = nc.dram_tensor("sc_ple", [256, D], BF16).ap()

    es = contextlib.ExitStack()
    with es:
        sems = [es.enter_context(nc.semaphore(f"s{i}")) for i in range(96)]
        S = Sched(nc, sems)
        big = es.enter_context(nc.sbuf_tensor("big", [128, 212800 // 4], F32))
        pp = es.enter_context(nc.psum_tensor("pp", [128, 4096], F32))

        class Arena:
            def __init__(self, base, limit):
                self.p, self.limit = base, limit

            def take(self, nbytes, dt, pattern=None, **kw):
                assert nbytes % 4 == 0
                off = self.p
                self.p += (nbytes + 31) // 32 * 32
                assert self.p <= self.limit, (self.p, self.limit)
                v = big[:, off // 4:(off + nbytes) // 4]
                if dt != F32:
                    v = v.bitcast(dt)
                if pattern:
                    v = v.rearrange(pattern, **kw)
                return v

        TOTAL = 212800
        A = Arena(0, TOTAL)
        ident_bf = A.take(256, BF16)
        ident_f = A.take(512, F32)
        tri = A.take(256, BF16)
        nhalf = A.take(32, F32)
        biasT = A.take(4 * 16 * 4, F32, "p (h d) -> p h d", h=4)
        lam = A.take(32, F32)
        gsub = A.take(512, F32)
        chanP = A.take(4 * 34 * 4, F32, "p (c j) -> p c j", c=4)
        small = A.take(2048, F32)
        NS = 4
        kpg = [A.take(1024, BF16) for _ in range(NS)]
        vpg = [A.take(1024, BF16) for _ in range(NS)]
        kTp = [A.take(1024, BF16, "p (h t) -> p h t", h=4) for _ in range(2)]
        PTs = [A.take(64, BF16) for _ in range(2)]
        qblk = A.take(4 * 4 * 8 * 2, BF16, "p (b h j) -> p b h j", b=4, h=4)
        us_ext = A.take(4 * 4 * 34 * 4, F32, "p (c b j) -> p c b j", c=4, b=4)
        idx_i = A.take(256 * 4, I32)
        pt_i = A.take(256 * 4, I32)
        pt_f = A.take(256 * 4, F32)
        idx_f = pt_f
        iop_f = A.take(32, F32)
        Atab = A.take(3 * 128 * 2, BF16, "p (r t) -> p r t", r=3)
        Btab = A.take(65 * 32 * 2, BF16, "p (j c) -> p j c", j=65)
        smask = A.take(128 * 2, BF16)
        Osb = A.take(4 * 129 * 4, F32, "p (h e) -> p h e", h=4)
        Dmat = A.take(32, F32)
        s_tmp = A.take(5504, F32)
        XY = A.take(8 * NC_ * 2, BF16, "p (c n) -> p c n", c=8)
        pers_end = A.p
        qT = A.take(4 * NC_ * 2, BF16, "p (h n) -> p h n", h=4)
        kT = A.take(4 * NC_ * 2, BF16, "p (h n) -> p h n", h=4)
        v_ext = A.take(NT * 4 * 129 * 2, BF16, "p (i h e) -> p i h e", i=NT, h=4)
        UW = 30 + NC_
        uT = A.take(4 * UW * 4, F32, "p (c n) -> p c n", c=4)
        qkvu_end = A.p
        r1_base = A.p
        xs = A.take(4096, F32)
        xst = [A.take(4096, F32) for _ in range(2)]
        xn = [A.take(2048, BF16) for _ in range(2)]
        wst = [A.take(8192, BF16, "p (c n) -> p c n", c=8) for _ in range(2)]
        kvf = [A.take(2048, F32) for _ in range(3)]
        kbf = [A.take(1024, BF16) for _ in range(2)]
        gpre = A.take(4096, F32)
        gth = [A.take(2048, F32) for _ in range(2)]
        gah = [A.take(2048, F32) for _ in range(2)]
        junk = A.take(2048, BF16)
        m1_end = A.p
        A2 = Arena(r1_base, TOTAL)
        y_attn = A2.take(NT * 512 * 2, BF16, "p (i n) -> p i n", i=NT)
        cacc = A2.take(4 * 512 * 4, F32, "p (c n) -> p c n", c=4)
        zbuf = [A2.take(2048, F32) for _ in range(2)]
        PT = [A2.take(512, BF16) for _ in range(3)]
        o0 = [A2.take(1024, F32, "p (q e) -> p q e", q=2) for _ in range(2)]
        t1 = A2.take(1024, F32, "p (q e) -> p q e", q=2)
        osum = A2.take(1024, F32, "p (q e) -> p q e", q=2)
        osq = A2.take(1024, F32, "p (q e) -> p q e", q=2)
        sth = [A2.take(2048, F32, "p (c n) -> p c n", c=4) for _ in range(2)]
        szz = [A2.take(2048, F32, "p (c n) -> p c n", c=4) for _ in range(2)]
        sprod = A2.take(4 * 4 * 31 * 4, F32, "p (b t j) -> p b t j", b=4, t=4)
        accS = A2.take(4 * 4 * 4 * 4, F32, "p (c b t) -> p c b t", c=4, b=4)
        ststage = A2.take(2048, F32)
        cstage = A2.take(2048, F32)
        ustm = A2.take(2048, F32)
        junk2 = A2.take(2048, BF16)
        A3 = Arena(pers_end, TOTAL)
        hblk = A3.take(4 * 4096, F32, "p (i n) -> p i n", i=4)
        hnT = A3.take(8 * 512 * 2, BF16, "p (c n) -> p c n", c=8)
        f1T = A3.take(32 * 512 * 2, BF16, "p (c n) -> p c n", c=32)
        wsD = [A3.take(8192, BF16) for _ in range(4)]
        xrD = [A3.take(4096, F32) for _ in range(2)]
        prD = [A3.take(1024, F32) for _ in range(2)]
        pbD = [A3.take(512, BF16) for _ in range(2)]
        peT = [A3.take(512, BF16, "p (c n) -> p c n", c=2) for _ in range(2)]
        tmpD = [A3.take(4096, F32) for _ in range(3)]
        hbfD = [A3.take(2048, BF16) for _ in range(2)]
        gbc = [A3.take(4096, F32) for _ in range(3)]
        junk3 = A3.take(2048, BF16)
        fblk = A3.take(4 * 2048, F32, "p (i n) -> p i n", i=4)

        def bank(b, n=1):
            return pp[:, b * 512:(b + n) * 512]

        RB = [Res(f"bank{b}") for b in range(8)]

        def bank_bf(b):
            return bank(b).bitcast(BF16)

        def R(name="", const=False):
            return Res(name, const)

        def OP(eng, fn, r=(), w=()):
            return S.op(eng, fn, r, w)

        def MM(out, lhsT, rhs, start, stop, r, w, **kw):
            return S.op("pe", lambda e: e.matmul(out, lhsT=lhsT, rhs=rhs, start=start, stop=stop, **kw), r, w)

        def TR(out, in_, ident, r, w):
            return S.op("pe", lambda e: e.transpose(out=out, in_=in_, identity=ident), r, w)

        def DMA(q, out, in_, slot, r=(), w=(), is_output=False, **kw):
            return S.dma(q, lambda e: e.dma_start(out=out, in_=in_, **kw), slot, r, w, is_output)

        so = S.slot(group=True)

        Rc = {n: R(n, True) for n in ["ident_bf", "ident_f", "tri", "nhalf", "biasT", "lam", "gsub", "chanP",
                                      "iop", "Atab", "Btab", "smask", "Dmat", "gpre", "ones"]}
        OP("pool", lambda e: e.memset(ident_bf, 0.0), w=[Rc["ident_bf"]])
        OP("pool", lambda e: e.affine_select(out=ident_bf, in_=ident_bf, compare_op=ALU.not_equal, fill=1.0,
                                             base=0, pattern=[[-1, 128]], channel_multiplier=1),
           r=[Rc["ident_bf"]], w=[Rc["ident_bf"]])
        OP("pool", lambda e: e.memset(ident_f, 0.0), w=[Rc["ident_f"]])
        OP("pool", lambda e: e.affine_select(out=ident_f, in_=ident_f, compare_op=ALU.not_equal, fill=1.0,
                                             base=0, pattern=[[-1, 128]], channel_multiplier=1),
           r=[Rc["ident_f"]], w=[Rc["ident_f"]])
        OP("pool", lambda e: e.memset(tri, 1.0), w=[Rc["tri"]])
        OP("pool", lambda e: e.affine_select(out=tri, in_=tri, compare_op=ALU.is_ge, fill=0.0,
                                             base=0, pattern=[[1, 128]], channel_multiplier=-1),
           r=[Rc["tri"]], w=[Rc["tri"]])
        OP("pool", lambda e: e.memset(nhalf, -0.5), w=[Rc["nhalf"]])
        iot = small[:, 0:16]
        Riot = R("iot")
        OP("pool", lambda e: e.iota(iot, pattern=[[128, 16]], base=-128 * 15, channel_multiplier=1,
                                    allow_small_or_imprecise_dtypes=True), w=[Riot])
        for h in range(4):
            OP("pool", lambda e, h=h: e.tensor_scalar(out=biasT[:, h, :], in0=iot, scalar1=SLOPES[h], scalar2=None,
                                                      op0=ALU.mult), r=[Riot], w=[Rc["biasT"]])
        OP("pool", lambda e: e.iota(iop_f, pattern=[[0, 8]], base=0, channel_multiplier=1,
                                    allow_small_or_imprecise_dtypes=True), w=[Rc["iop"]])

        if stop_after == "p0a":
            S.barrier()
            S.emit()
            return nc
        pslot = S.slot(group=True)
        lamv = small[:, 16:16 + 256].rearrange("p (a d) -> p a d", a=4)
        Rlamv = R("lamv")
        for i, ap in enumerate([lamq1, lamk1, lamq2, lamk2]):
            DMA("sp", lamv[:, i, :], ap.partition_broadcast(128), pslot, w=[Rlamv])
        Rgs = R("gsraw")
        DMA("sp", gsub, g_subln.partition_broadcast(128), pslot, w=[Rgs])
        DMA("sp", gpre, g_pre_mix.partition_broadcast(128), pslot, w=[Rc["gpre"]])
        ptm = xst[1]
        Rptm = R("ptm")
        DMA("sp", ptm[0:31, 0:512], w_dw, pslot, w=[Rptm])
        DMA("sp", ptm[31:32, 0:512], b_dw, pslot, w=[Rptm])
        DMA("sp", ptm[32:33, 0:512], ln_g, pslot, w=[Rptm])
        DMA("sp", ptm[33:34, 0:512], ln_b, pslot, w=[Rptm])
        if stop_after == "p0b":
            S.barrier()
            S.emit()
            return nc
        lprod = small[:, 16 + 256:16 + 256 + 128].rearrange("p (a d) -> p a d", a=2)
        Rlp = R("lprod")
        OP("dve", lambda e: e.tensor_tensor(out=lprod[:, 0, :], in0=lamv[:, 0, :], in1=lamv[:, 1, :], op=ALU.mult),
           r=[Rlamv], w=[Rlp])
        OP("dve", lambda e: e.tensor_tensor(out=lprod[:, 1, :], in0=lamv[:, 2, :], in1=lamv[:, 3, :], op=ALU.mult),
           r=[Rlamv], w=[Rlp])
        lsum = small[:, 400:402]
        Rls = R("lsum")
        OP("dve", lambda e: e.tensor_reduce(out=lsum, in_=lprod, axis=AX.X, op=ALU.add), r=[Rlp], w=[Rls])
        lexp = small[:, 402:404]
        Rle = R("lexp")
        OP("act", lambda e: e.activation(out=lexp, in_=lsum, func=AF.Exp), r=[Rls], w=[Rle])
        OP("dve", lambda e: e.tensor_tensor(out=lam[:, 0:1], in0=lexp[:, 0:1], in1=lexp[:, 1:2], op=ALU.subtract),
           r=[Rle], w=[Rc["lam"]])
        OP("dve", lambda e: e.tensor_scalar(out=lam[:, 0:1], in0=lam[:, 0:1], scalar1=LAM_INIT, scalar2=None,
                                            op0=ALU.add), r=[Rc["lam"]], w=[Rc["lam"]])
        OP("dve", lambda e: e.tensor_scalar(out=lam[:, 1:2], in0=lam[:, 0:1], scalar1=-1.0, scalar2=None,
                                            op0=ALU.mult), r=[Rc["lam"]], w=[Rc["lam"]])
        OP("dve", lambda e: e.tensor_scalar(out=gsub, in0=gsub, scalar1=1.0 - LAM_INIT, scalar2=None, op0=ALU.mult),
           r=[Rgs], w=[Rc["gsub"]])
        if stop_after == "p0c":
            S.barrier()
            S.emit()
            return nc
        cpb = bank(0)[:, 0:4 * 34].rearrange("p (c j) -> p c j", c=4)
        for ct in range(4):
            TR(cpb[:, ct, :], ptm[0:34, ct * 128:(ct + 1) * 128], ident_f[0:34, 0:34],
               r=[Rptm, Rc["ident_f"]], w=[RB[0]])
        OP("dve", lambda e: e.tensor_copy(out=chanP, in_=cpb), r=[], w=[RB[0], Rc["chanP"]])
        OP("dve", lambda e: e.tensor_scalar(out=chanP[:, :, 32:34], in0=chanP[:, :, 32:34], scalar1=0.5, scalar2=None,
                                            op0=ALU.mult), r=[Rc["chanP"]], w=[Rc["chanP"]])

        if stop_after == "p0d":
            S.barrier()
            S.emit()
            return nc
        wslot = S.slot(group=True)
        Rsc = {n: R("sc_" + n, True) for n in ["out", "ff1", "ff2", "gate", "ple"]}

        def v2k(ap):
            R_, C_ = ap.shape
            if C_ >= 2048:
                return ap.rearrange("r (a b) -> (r a) b", b=2048)
            return ap.rearrange("(r a) c -> r (a c)", a=2048 // C_)

        def conv_scratch():
            DMA("pool", v2k(sc_out), v2k(w_out), wslot, w=[Rsc["out"]])
            a1, b1 = v2k(sc_ff1), v2k(w_ff1)
            for i in range(4):
                DMA("pool", a1[i * 512:(i + 1) * 512, :], b1[i * 512:(i + 1) * 512, :], wslot, w=[Rsc["ff1"]])
            a2, b2_ = v2k(sc_ff2), v2k(w_ff2)
            for i in range(4):
                DMA("pool", a2[i * 512:(i + 1) * 512, :], b2_[i * 512:(i + 1) * 512, :], wslot, w=[Rsc["ff2"]])
            DMA("pool", v2k(sc_gate), v2k(w_gate), wslot, w=[Rsc["gate"]])
            DMA("pool", v2k(sc_ple), v2k(w_ple), wslot, w=[Rsc["ple"]])

        WGROUPS = [
            [(0, 512)], [(512, 512)], [(1024, 512)],
            [(1536, 128), (2048, 128), (1664, 128), (2176, 128)],
            [(1792, 128), (2304, 128), (1920, 128), (2432, 128)],
        ]
        Rw = [R("wst0"), R("wst1")]
        wsl = [S.slot(), S.slot()]
        w_in_v = w_in.rearrange("(c p) n -> p c n", p=128)

        def load_wgroup(g):
            b = g % 2
            o = 0
            for (c0, n) in WGROUPS[g]:
                DMA("pool", wst[b][:, :, o:o + n], w_in_v[:, :, c0:c0 + n], wsl[b], w=[Rw[b]])
                o += n

        load_wgroup(0)
        load_wgroup(1)

        if stop_after == "p0":
            S.barrier()
            S.emit()
            return nc
        xnT = XY
        Rxst = [R("xst0"), R("xst1"), R("xst2")]
        assert kvf[1].offset == kvf[0].offset + 512
        xst = list(xst) + [bass.AP(kvf[0].tensor, kvf[0].offset, [list(kvf[0].ap[0]), [1, 1024]])]
        Rxs = R("xs")
        xsl = [S.slot(), S.slot(), S.slot()]
        xsl3 = S.slot()
        Rxn = [R("xn0"), R("xn1")]
        RxnT = [R(f"xnT{i}") for i in range(NT)]
        Rjunk = R("junk")
        stat = small[:, 404:404 + 32]
        Rstat = [R(f"stat{i}") for i in range(8)]

        def rstd_chain(src_ap, src_res, n, slot_i, scale):
            c = stat[:, slot_i * 4:slot_i * 4 + 4]
            rs = Rstat[slot_i]
            OP("act", lambda e: e.activation(out=junk[:, 0:n], in_=src_ap, func=AF.Square, accum_out=c[:, 0:1]),
               r=[src_res], w=[Rjunk, rs])
            OP("dve", lambda e: e.tensor_scalar(out=c[:, 1:2], in0=c[:, 0:1], scalar1=scale, scalar2=EPS,
                                                op0=ALU.mult, op1=ALU.add), r=[rs], w=[rs])
            OP("pool", lambda e: e.tensor_tensor(out=c[:, 2:3], in0=c[:, 1:2], in1=nhalf[:, 0:1], op=ALU.pow),
               r=[rs, Rc["nhalf"]], w=[rs])
            return c[:, 2:3], rs

        OP("pool", lambda e: e.memset(xs, 0.0), w=[Rxs])
        for bl in range(4):
            DMA("sp", xs[bl * 32:bl * 32 + 4, :], x_s[bl * 4:bl * 4 + 4, :], xsl[2], w=[Rxs])

        tile_order = [16] + list(range(16))
        for n_i, i in enumerate(tile_order):
            if i == 16:
                xt, rx = xs, Rxs
            else:
                b = i % 3
                xt, rx = xst[b], Rxst[b]
                xslot = xsl[b] if b < 2 else xsl3
                if i == 1:
                    DMA("sp", xt, x_p[i * 128:(i + 1) * 128, :], xslot, r=[], w=[rx, Rptm])
                else:
                    DMA("sp", xt, x_p[i * 128:(i + 1) * 128, :], xslot, w=[rx])
            rstd, rs = rstd_chain(xt, rx, 1024, n_i % 4, 1.0 / D)
            b2 = n_i % 2
            OP("dve", lambda e, xt=xt, rstd=rstd, b2=b2: e.scalar_tensor_tensor(
                out=xn[b2], in0=xt, scalar=rstd, in1=gpre, op0=ALU.mult, op1=ALU.mult),
               r=[rx, rs, Rc["gpre"]], w=[Rxn[b2]])
            pb = n_i % 2
            tb = bank_bf(pb)[:, 0:1024].rearrange("p (c n) -> p c n", c=8)
            for c in range(8):
                TR(tb[:, c, :], xn[b2][:, c * 128:(c + 1) * 128], ident_bf, r=[Rxn[b2], Rc["ident_bf"]], w=[RB[pb]])
            OP("act", lambda e, i=i, tb=tb: e.copy(out=xnT[:, :, i * 128:(i + 1) * 128], in_=tb),
               r=[], w=[RB[pb], RxnT[i]])

        if stop_after == "p1":
            S.barrier()
            S.emit()
            return nc
        COLB = [(0, 512), (512, 512), (1024, 512), (1536, 512), (2048, 128)]

        def tiles_of(cb):
            c0, n = cb
            return list(range(c0 // 128, (c0 + n) // 128))

        RqT = [R(f"qT{i}") for i in range(NT)]
        RkT = [R(f"kT{i}") for i in range(NT)]
        Rv = [R(f"v{i}") for i in range(NT)]
        RuT = [R(f"uT{i}") for i in range(NT)]
        Rupad = R("upad")
        Rkvf = [R(f"kvf{i}") for i in range(3)]
        kvsl = [S.slot() for _ in range(3)]
        Rkbf = [R("kbf0"), R("kbf1")]
        Rgth = [R("gth0"), R("gth1")]
        Rgah = [R("gah0"), R("gah1")]
        nb = [0]

        nbmod = [8]

        def nextbank():
            nb[0] = (nb[0] + 1) % nbmod[0]
            return nb[0]

        OP("pool", lambda e: e.memset(v_ext[:, :, :, 128:129], 1.0), w=[Rc["ones"]])
        OP("pool", lambda e: e.memset(uT[:, :, 0:30], 0.0), w=[Rupad])

        wq = wst[0]
        for h in range(4):
            for cb in COLB:
                c0, n = cb
                b = nextbank()
                for c in range(8):
                    MM(bank(b)[:, 0:n], wq[:, c, h * 128:(h + 1) * 128], xnT[:, c, c0:c0 + n], c == 0, c == 7,
                       r=[Rw[0]] + [RxnT[i] for i in tiles_of(cb)], w=[RB[b]])
                eng = "act" if (h + c0 // 512) % 2 == 0 else "dve"
                if eng == "act":
                    OP("act", lambda e, b=b, h=h, c0=c0, n=n: e.copy(out=qT[:, h, c0:c0 + n], in_=bank(b)[:, 0:n]),
                       w=[RB[b]] + [RqT[i] for i in tiles_of(cb)])
                else:
                    OP("dve", lambda e, b=b, h=h, c0=c0, n=n: e.tensor_copy(out=qT[:, h, c0:c0 + n],
                                                                           in_=bank(b)[:, 0:n]),
                       w=[RB[b]] + [RqT[i] for i in tiles_of(cb)])
        load_wgroup(2)

        def out_rows(dst_p, dst_s, i, src, slot, rres):
            if i < 16:
                DMA("sp", dst_p[i * 128:(i + 1) * 128, :], src, slot, r=[rres], is_output=True)
            else:
                for bl in range(4):
                    DMA("sp", dst_s[bl * 4:bl * 4 + 4, :], src[bl * 32:bl * 32 + 4, :], slot, r=[rres],
                        is_output=True)

        wk = wst[1]
        for n_i, i in enumerate(tile_order):
            b = nextbank()
            for c in range(8):
                MM(bank(b), xnT[:, c, i * 128:(i + 1) * 128], wk[:, c, :], c == 0, c == 7,
                   r=[Rw[1], RxnT[i]], w=[RB[b]])
            s4 = n_i % 3
            OP("act", lambda e, b=b, s4=s4: e.copy(out=kvf[s4], in_=bank(b)), w=[RB[b], Rkvf[s4]])
            s2 = n_i % 2
            OP("dve", lambda e, b=b, s2=s2: e.tensor_copy(out=kbf[s2], in_=bank(b)), w=[RB[b], Rkbf[s2]])
            out_rows(k_p, k_s, i, kvf[s4], kvsl[s4], Rkvf[s4])
            b2 = nextbank()
            tb = bank_bf(b2)[:, 0:512].rearrange("p (h n) -> p h n", h=4)
            for h in range(4):
                TR(tb[:, h, :], kbf[s2][:, h * 128:(h + 1) * 128], ident_bf, r=[Rkbf[s2], Rc["ident_bf"]],
                   w=[RB[b2]])
            OP("dve", lambda e, i=i, tb=tb: e.tensor_copy(out=kT[:, :, i * 128:(i + 1) * 128], in_=tb),
               w=[RB[b2], RkT[i]])
        load_wgroup(3)

        wv = wst[0]
        for n_i, i in enumerate(tile_order):
            b = nextbank()
            for c in range(8):
                MM(bank(b), xnT[:, c, i * 128:(i + 1) * 128], wv[:, c, :], c == 0, c == 7,
                   r=[Rw[0], RxnT[i]], w=[RB[b]])
            s4 = (n_i + 1) % 3
            OP("act", lambda e, b=b, s4=s4: e.copy(out=kvf[s4], in_=bank(b)), w=[RB[b], Rkvf[s4]])
            OP("dve", lambda e, b=b, i=i: e.tensor_copy(out=v_ext[:, i, :, 0:128],
                                                        in_=bank(b).rearrange("p (h e) -> p h e", h=4)),
               w=[RB[b], Rv[i]])
            out_rows(v_p, v_s, i, kvf[s4], kvsl[s4], Rkvf[s4])
        load_wgroup(4)
        conv_scratch()

        for g in (3, 4):
            wg_ = wst[g % 2]
            for j in range(2):
                ct = (g - 3) * 2 + j
                for cb in COLB:
                    c0, n = cb
                    ba = nextbank()
                    for c in range(8):
                        MM(bank(ba)[:, 0:n], wg_[:, c, j * 256:j * 256 + 128], xnT[:, c, c0:c0 + n], c == 0, c == 7,
                           r=[Rw[g % 2]] + [RxnT[i] for i in tiles_of(cb)], w=[RB[ba]])
                    bg = nextbank()
                    for c in range(8):
                        MM(bank(bg)[:, 0:n], wg_[:, c, j * 256 + 128:j * 256 + 256], xnT[:, c, c0:c0 + n], c == 0,
                           c == 7, r=[Rw[g % 2]] + [RxnT[i] for i in tiles_of(cb)], w=[RB[bg]])
                    s2 = (c0 // 512) % 2
                    OP("act", lambda e, bg=bg, n=n, s2=s2: e.activation(out=gth[s2][:, 0:n], in_=bank(bg)[:, 0:n],
                                                                        func=AF.Tanh, scale=0.5),
                       w=[RB[bg], Rgth[s2]])
                    OP("act", lambda e, ba=ba, n=n, s2=s2: e.activation(out=gah[s2][:, 0:n], in_=bank(ba)[:, 0:n],
                                                                        func=AF.Copy, scale=0.5),
                       w=[RB[ba], Rgah[s2]])
                    OP("dve", lambda e, ct=ct, c0=c0, n=n, s2=s2: e.scalar_tensor_tensor(
                        out=uT[:, ct, 30 + c0:30 + c0 + n], in0=gth[s2][:, 0:n], scalar=1.0, in1=gah[s2][:, 0:n],
                        op0=ALU.add, op1=ALU.mult),
                       r=[Rgth[s2], Rgah[s2]], w=[RuT[i] for i in tiles_of(cb)])


        if stop_after in ("inproj", "p2s1", "p2s2", "p2s3"):
            S.barrier()
            S.emit()
            return nc


        def finish():
            if dbg_ym is not None and stop_after not in ("d1", "d3", "d4"):
                S.barrier()
                dsl = S.slot()
                S.dma("sp", lambda e: e.dma_start(out=dbg_ym, in_=XY.rearrange("p c n -> p (c n)")), dsl,
                      is_output=True)
            S.barrier()
            S.emit()
            return nc

        nbmod[0] = 4
        S.barrier()
        ymixT = XY
        Rym = [R(f"ym{i}") for i in range(NT)]
        Rymc = [R(f"ymc{i}") for i in range(NT)]
        epi = small[:, 440:440 + 24]
        Repi = R("epi")
        stsl = S.slot()
        gsl = S.slot()

        def mkap(base, dims):
            part = list(base.ap[0])
            return bass.AP(base.tensor, base.offset, [part] + [[s, c] for (s, c) in dims])

        Rcst = R("cstage")
        b = nextbank()
        cpv = bank(b)[0:30, :].rearrange("p (c n) -> p c n", c=4)
        for ct in range(4):
            TR(cpv[:, ct, :], uT[:, ct, 2048:2078], ident_f, r=[RuT[15], Rc["ident_f"]], w=[RB[b]])
        OP("act", lambda e, b=b: e.copy(out=cstage[0:30, 0:512], in_=bank(b)[0:30, :]), w=[RB[b], Rcst])
        DMA("sp", c_p, cstage[0:30, 0:512], gsl, r=[Rcst], is_output=True)
        Rsts = R("ststage")
        Rus = R("us_ext")
        DMA("sp", ststage[0:120, 0:512], st_conv, stsl, w=[Rsts])
        b = nextbank()
        for ct in range(4):
            TR(bank(b)[:, ct * 120:(ct + 1) * 120], ststage[0:120, ct * 128:(ct + 1) * 128], ident_f[0:120, 0:120],
               r=[Rsts, Rc["ident_f"]], w=[RB[b]])
        OP("dve", lambda e, b=b: e.tensor_copy(
            out=us_ext[:, :, :, 0:30], in_=bank(b)[:, 0:480].rearrange("p (c b j) -> p c b j", c=4, b=4)),
           w=[RB[b], Rus])
        u_s_view = uT[:, :, 30 + 2048:30 + 2176].rearrange("p c (b j) -> p c b j", b=4)[:, :, :, 0:4]
        OP("dve", lambda e: e.tensor_copy(out=us_ext[:, :, :, 30:34], in_=u_s_view), r=[RuT[16]], w=[Rus])
        for bl in range(4):
            DMA("sp", c_s[bl * 30:bl * 30 + 26, :], st_conv[bl * 30 + 4:bl * 30 + 30, :], gsl, is_output=True)
        Rustm = R("ustm")
        b = nextbank()
        for ct in range(4):
            TR(bank(b)[:, ct * 128:(ct + 1) * 128], uT[:, ct, 30 + 2048:30 + 2176], ident_f,
               r=[RuT[16], Rc["ident_f"]], w=[RB[b]])
        OP("act", lambda e, b=b: e.copy(out=ustm[:, 0:512], in_=bank(b)), w=[RB[b], Rustm])
        for bl in range(4):
            DMA("sp", c_s[bl * 30 + 26:bl * 30 + 30, :], ustm[bl * 32:bl * 32 + 4, 0:512], gsl, r=[Rustm],
                is_output=True)
        RaccS = R("accS")
        Rsp = R("sprod")
        for ct in range(4):
            base = us_ext[:, ct, :, :]
            win = mkap(base, [(34, 4), (1, 4), (1, 31)])
            wb = mkap(chanP[:, ct, 0:31], [(0, 4), (0, 4), (1, 31)])
            OP("dve", lambda e, win=win, wb=wb: e.tensor_tensor(out=sprod, in0=win, in1=wb, op=ALU.mult),
               r=[Rus, Rc["chanP"]], w=[Rsp])
            OP("dve", lambda e, ct=ct: e.tensor_reduce(out=accS[:, ct, :, :], in_=sprod, axis=AX.X, op=ALU.add),
               r=[Rsp], w=[RaccS])
            OP("dve", lambda e, ct=ct: e.tensor_scalar(out=accS[:, ct, :, :], in0=accS[:, ct, :, :],
                                                       scalar1=chanP[:, ct, 31:32], scalar2=None, op0=ALU.add),
               r=[RaccS, Rc["chanP"]], w=[RaccS])

        if stop_after == "convout":
            return finish()

        Rpt = R("pt")
        ptsl = S.slot()
        DMA("sp", pt_i, ptab.partition_broadcast(128), ptsl, w=[Rpt])
        Ridx = R("idx", True)
        OP("dve", lambda e: e.tensor_copy(out=pt_f, in_=pt_i), r=[Rpt], w=[Rpt])
        OP("dve", lambda e: e.tensor_scalar(out=idx_f, in0=pt_f, scalar1=128.0, scalar2=iop_f[:, 0:1],
                                            op0=ALU.mult, op1=ALU.add), r=[Rpt, Rc["iop"]], w=[Rpt])
        OP("dve", lambda e: e.tensor_copy(out=idx_i, in_=idx_f), r=[Rpt], w=[Ridx])
        Rqb = R("qblk", True)
        OP("pool", lambda e: e.memset(qblk, 0.0), w=[Rqb])
        for c in range(2):
            qsrc = mkap(qT[c * 64:(c + 1) * 64, 0, 2048:2052], [(32, 4), (NC_, 4), (1, 4)])
            OP("dve", lambda e, c=c, qsrc=qsrc: e.tensor_copy(out=qblk[c * 64:(c + 1) * 64, :, :, c * 4:(c + 1) * 4],
                                                             in_=qsrc), r=[RqT[16]], w=[Rqb])
        Rstmp = R("s_tmp")
        tA = s_tmp[:, 0:256].bitcast(BF16).rearrange("p (r t) -> p r t", r=2)
        RtA = Rstmp
        OP("pool", lambda e: e.iota(Atab[0:1, 0, :], pattern=[[1, 128]], base=0, channel_multiplier=0,
                                    allow_small_or_imprecise_dtypes=True), w=[Rc["Atab"]])
        OP("pool", lambda e: e.iota(Atab[0:1, 1, :], pattern=[[0, 4], [1, 32]], base=0, channel_multiplier=0,
                                    allow_small_or_imprecise_dtypes=True), w=[Rc["Atab"]])
        OP("pool", lambda e: e.memset(tA[0:1, 0, :], 1.0), w=[RtA])
        tsl = S.slot()
        DMA("sp", Atab[1:2, 0:2, :], tA[0:1, 0, :].rearrange("p (r t) -> p r t", r=2), tsl, r=[RtA], w=[Rc["Atab"]])
        tB = s_tmp[:, 256:256 + 65 * 16].bitcast(BF16).rearrange("p (j c) -> p j c", j=65)
        tJ = s_tmp[:, 1296:1296 + 65]
        RtB = Rstmp
        RtJ = Rstmp
        OP("pool", lambda e: e.iota(tJ[0:1, :], pattern=[[1, 65]], base=-64, channel_multiplier=0,
                                    allow_small_or_imprecise_dtypes=True), w=[RtJ])
        for h in range(4):
            OP("pool", lambda e, h=h: e.memset(Btab[0:1, :, h * 8:(h + 1) * 8], 8.0 * SLOPES[h]), w=[Rc["Btab"]])
            OP("pool", lambda e, h=h: e.tensor_scalar(
                out=tB[0:1, :, h * 8:(h + 1) * 8], in0=tJ[0:1, :].unsqueeze(2).to_broadcast([1, 65, 8]),
                scalar1=1024.0 * SLOPES[h], scalar2=None, op0=ALU.mult), r=[RtJ], w=[RtB])
        DMA("sp", Btab[1:2, :, :], tB[0:1, :, :], tsl, r=[RtB], w=[Rc["Btab"]])
        sm4 = smask.rearrange("p (b g t) -> p b g t", b=4, g=8)
        OP("pool", lambda e: e.memset(smask, 1.0), w=[Rc["smask"]])
        OP("pool", lambda e: e.affine_select(out=sm4, in_=sm4, compare_op=ALU.is_ge, fill=0.0, base=0,
                                             pattern=[[-32, 4], [0, 8], [0, 4]], channel_multiplier=1),
           r=[Rc["smask"]], w=[Rc["smask"]])
        OP("pool", lambda e: e.affine_select(out=sm4, in_=sm4, compare_op=ALU.is_ge, fill=0.0, base=0,
                                             pattern=[[32, 4], [0, 8], [1, 4]], channel_multiplier=-1),
           r=[Rc["smask"]], w=[Rc["smask"]])
        Dpos = s_tmp[0:8, 1364:1368]
        Dneg = s_tmp[0:8, 1368:1372]
        RD = Rstmp
        OP("pool", lambda e: e.memset(s_tmp[0:8, 1364:1372], 1.0), w=[RD])
        OP("pool", lambda e: e.affine_select(out=Dpos, in_=Dpos, compare_op=ALU.is_equal, fill=0.0, base=0,
                                             pattern=[[-1, 4]], channel_multiplier=1), r=[RD], w=[RD])
        OP("pool", lambda e: e.affine_select(out=Dneg, in_=Dneg, compare_op=ALU.is_equal, fill=0.0, base=-4,
                                             pattern=[[-1, 4]], channel_multiplier=1), r=[RD], w=[RD])
        OP("dve", lambda e: e.scalar_tensor_tensor(out=Dmat[0:8, 0:4], in0=Dneg, scalar=lam[0:8, 1:2], in1=Dpos,
                                                   op0=ALU.mult, op1=ALU.add), r=[RD, Rc["lam"]], w=[Rc["Dmat"]])
        Rkpg = [R(f"kpg{s}") for s in range(NS)]
        Rvpg = [R(f"vpg{s}") for s in range(NS)]
        ksl = [S.slot() for _ in range(NS)]
        vsl = [S.slot() for _ in range(NS)]
        RkTp = [R("kTp0"), R("kTp1")]
        RPTs = [R("PTs0"), R("PTs1")]
        ROsb = R("Osb")
        cache_v3 = cache_v.rearrange("r (h e) -> r h e", h=4)
        MB = [6, 7]
        OB = [4, 5]

        def page_dma(n):
            bl, j = divmod(n, NPG)
            s = n % NS
            col = bl * NPG + j
            S.dma("pool", lambda e: e.indirect_dma_start(
                out=kpg[s], out_offset=None, in_=cache_k,
                in_offset=bass.IndirectOffsetOnAxis(ap=idx_i[:, col:col + 1], axis=0)),
                ksl[s], reads=[Ridx], writes=[Rkpg[s]])
            S.dma("pool", lambda e: e.indirect_dma_start(
                out=vpg[s], out_offset=None, in_=cache_v,
                in_offset=bass.IndirectOffsetOnAxis(ap=idx_i[:, col:col + 1], axis=0)),
                vsl[s], reads=[Ridx], writes=[Rvpg[s]])

        PAGES = [(bl, j) for bl in range(4) for j in range(NPG + 1)]
        NPT = 4 * NPG

        def stageA(k):
            bl, j = PAGES[k]
            if j == NPG:
                return
            a = k % 2
            mb = MB[a]
            n = bl * NPG + j
            s = n % NS
            ktp = bank_bf(mb)[:, 0:512].rearrange("p (h t) -> p h t", h=4)
            for h in range(4):
                TR(ktp[:, h, :], kpg[s][:, h * 128:(h + 1) * 128], ident_bf,
                   r=[Rkpg[s], Rc["ident_bf"]], w=[RB[mb]])
            if k % 2 == 0:
                OP("act", lambda e, a=a, ktp=ktp: e.copy(out=kTp[a], in_=ktp), w=[RB[mb], RkTp[a]])
            else:
                OP("dve", lambda e, a=a, ktp=ktp: e.tensor_copy(out=kTp[a], in_=ktp), w=[RB[mb], RkTp[a]])

        def stageB(k):
            bl, j = PAGES[k]
            a = k % 2
            mb = MB[a]
            sT = bank(mb)[:, 256:288]
            new = (j == NPG)
            for h in range(4):
                if not new:
                    MM(sT[:, h * 8:(h + 1) * 8], kTp[a][:, h, :], qblk[:, bl, h, :], h == 0, False,
                       r=[RkTp[a], Rqb], w=[RB[mb]], skip_group_check=True)
                else:
                    MM(sT[:, h * 8:(h + 1) * 8], kT[:, h, 2048:2176], qblk[:, bl, h, :], h == 0, False,
                       r=[RkT[16], Rqb], w=[RB[mb]], skip_group_check=True)
            MM(sT, Atab[0:2, 1 if new else 0, :], Btab[0:2, j, :], False, True,
               r=[Rc["Atab"], Rc["Btab"]], w=[RB[mb]], skip_group_check=True)
            OP("act", lambda e, a=a, sT=sT: e.activation(out=PTs[a], in_=sT, func=AF.Exp, scale=0.125),
               w=[RB[mb], RPTs[a]])
            if new:
                OP("dve", lambda e, a=a, bl=bl: e.tensor_tensor(
                    out=PTs[a], in0=PTs[a], in1=smask[:, bl * 32:(bl + 1) * 32], op=ALU.mult),
                   r=[RPTs[a], Rc["smask"]], w=[RPTs[a]])

        def stageC(k):
            bl, j = PAGES[k]
            a = k % 2
            new = (j == NPG)
            if not new:
                n = bl * NPG + j
                s = n % NS
                vsrc = [vpg[s][:, h * 128:(h + 1) * 128] for h in range(4)]
                vres = Rvpg[s]
            else:
                vsrc = [v_ext[:, 16, h, 0:128] for h in range(4)]
                vres = Rv[16]
            for h in range(4):
                ob = OB[h // 2]
                ov = bank(ob)[0:8, 0:258].rearrange("p (q e) -> p q e", q=2)
                MM(ov[:, h % 2, 0:128], PTs[a][:, h * 8:(h + 1) * 8], vsrc[h], (j == 0 and h % 2 == 0), new,
                   r=[RPTs[a], vres], w=[RB[ob]], skip_group_check=True)
                MM(ov[:, h % 2, 128:129], PTs[a][:, h * 8:(h + 1) * 8], tri[:, 127:128], False, new,
                   r=[RPTs[a], Rc["tri"]], w=[RB[ob]], skip_group_check=True)
            if not new and n + NS < NPT:
                page_dma(n + NS)
            if new:
                for q2 in range(2):
                    OP("dve", lambda e, q2=q2: e.tensor_copy(
                        out=Osb[0:8, 2 * q2:2 * q2 + 2, :],
                        in_=bank(OB[q2])[0:8, 0:258].rearrange("p (q e) -> p q e", q=2)),
                       w=[RB[OB[q2]], ROsb])
                rl = s_tmp[0:8, 1280:1284]
                On = s_tmp[0:8, 0:512].rearrange("p (h e) -> p h e", h=4)
                OP("dve", lambda e: e.reciprocal(out=rl.unsqueeze(2), in_=Osb[0:8, :, 128:129]),
                   r=[ROsb], w=[Rstmp])
                OP("dve", lambda e: e.tensor_tensor(out=On, in0=Osb[0:8, :, 0:128],
                                                    in1=rl.unsqueeze(2).to_broadcast([8, 4, 128]), op=ALU.mult),
                   r=[ROsb, Rstmp], w=[Rstmp])
                mb2 = MB[k % 2]
                MM(bank(mb2)[0:4, :], Dmat[0:8, 0:4], s_tmp[0:8, 0:512], True, True,
                   r=[Rstmp, Rc["Dmat"]], w=[RB[mb2]])
                o_s = s_tmp[0:4, 512:1024].rearrange("p (h e) -> p h e", h=4)
                osq_s = s_tmp[0:4, 0:512].rearrange("p (h e) -> p h e", h=4)
                st4 = s_tmp[0:4, 1288:1300]
                OP("dve", lambda e, mb2=mb2: e.tensor_copy(out=s_tmp[0:4, 512:1024], in_=bank(mb2)[0:4, :]),
                   w=[RB[mb2], Rstmp])
                OP("dve", lambda e: e.tensor_tensor(out=osq_s, in0=o_s, in1=o_s, op=ALU.mult),
                   r=[Rstmp], w=[Rstmp])
                OP("dve", lambda e: e.tensor_reduce(out=st4[:, 0:4], in_=osq_s, axis=AX.X, op=ALU.add),
                   r=[Rstmp], w=[Rstmp])
                OP("dve", lambda e: e.tensor_scalar(out=st4[:, 4:8], in0=st4[:, 0:4], scalar1=1.0 / 128,
                                                    scalar2=EPS, op0=ALU.mult, op1=ALU.add),
                   r=[Rstmp], w=[Rstmp])
                OP("pool", lambda e: e.tensor_tensor(out=st4[:, 8:12], in0=st4[:, 4:8], in1=nhalf[0:4, 0:4],
                                                     op=ALU.pow), r=[Rstmp, Rc["nhalf"]], w=[Rstmp])
                OP("dve", lambda e: e.tensor_tensor(
                    out=osq_s, in0=o_s, in1=st4[:, 8:12].unsqueeze(2).to_broadcast([4, 4, 128]), op=ALU.mult),
                   r=[Rstmp], w=[Rstmp])
                ya_s = s_tmp[0:4, 1024:1024 + 256].bitcast(BF16).rearrange("p (h e) -> p h e", h=4)
                OP("dve", lambda e: e.tensor_tensor(
                    out=ya_s, in0=osq_s, in1=gsub[0:4, :].unsqueeze(1).to_broadcast([4, 4, 128]), op=ALU.mult),
                   r=[Rstmp, Rc["gsub"]], w=[Rstmp])
                mb3 = MB[(k + 1) % 2]
                tq = bank_bf(mb3)[:, 0:16].rearrange("p (h t) -> p h t", h=4)
                for h in range(4):
                    TR(tq[:, h, :], ya_s[:, h, :], ident_bf[0:4, 0:4], r=[Rstmp, Rc["ident_bf"]],
                       w=[RB[mb3]])
                OP("dve", lambda e, bl=bl, tq=tq: e.tensor_copy(
                    out=ymixT[:, 0:4, 2048 + bl * 32:2048 + bl * 32 + 4], in_=tq),
                   w=[RB[mb3], Rym[16]])

        def sample_gen():
            for n in range(min(NS, NPT)):
                page_dma(n)
            NP = len(PAGES)
            for t in range(NP + 2):
                if 0 <= t - 2 < NP:
                    stageC(t - 2)
                if 0 <= t - 1 < NP:
                    stageB(t - 1)
                if t < NP:
                    stageA(t)
                yield

        sgen = sample_gen()
        sdone = [False]

        def tick(k=1):
            for _ in range(k):
                if sdone[0]:
                    return
                try:
                    next(sgen)
                except StopIteration:
                    sdone[0] = True

        OP("pool", lambda e: e.memset(ymixT[:, :, 2048:2176], 0.0), w=[Rym[16], Rymc[16]])

        if stop_after == "sample":
            tick(10 ** 6)
            return finish()

        Rcacc = R("cacc")
        Rz = [R("z0"), R("z1")]
        Rsth = [R("sth0"), R("sth1")]
        Rszz = [R("szz0"), R("szz1")]
        lnst = small[:, 470:470 + 12]
        Rln = R("lnst")
        lni = [0]

        def ln_tile(i, cols):
            j = lni[0] % 2
            lni[0] += 1
            ba = 0
            for ct in range(4):
                TR(bank(ba)[:, ct * 128:(ct + 1) * 128], cacc[:, ct, cols:cols + 128], ident_f,
                   r=[Rcacc, Rc["ident_f"]], w=[RB[ba]])
            OP("dve", lambda e, ba=ba: e.bn_stats(out=lnst[:, 0:6], in_=bank(ba)), w=[RB[ba], Rln])
            OP("dve", lambda e: e.bn_aggr(out=lnst[:, 6:8], in_=lnst[:, 0:6]), r=[Rln], w=[Rln])
            OP("dve", lambda e: e.tensor_scalar(out=lnst[:, 8:9], in0=lnst[:, 7:8], scalar1=EPS, scalar2=None,
                                                op0=ALU.add), r=[Rln], w=[Rln])
            OP("pool", lambda e: e.tensor_tensor(out=lnst[:, 9:10], in0=lnst[:, 8:9], in1=nhalf[:, 0:1], op=ALU.pow),
               r=[Rln, Rc["nhalf"]], w=[Rln])
            OP("dve", lambda e: e.tensor_scalar(out=lnst[:, 10:11], in0=lnst[:, 6:7], scalar1=lnst[:, 9:10],
                                                scalar2=-1.0, op0=ALU.mult, op1=ALU.mult), r=[Rln], w=[Rln])
            OP("act", lambda e, ba=ba, j=j: e.activation(out=zbuf[j][:, 0:512], in_=bank(ba), func=AF.Identity,
                                                         bias=lnst[:, 10:11], scale=lnst[:, 9:10]),
               r=[Rln], w=[RB[ba], Rz[j]])
            yield
            yield
            yield
            bb = 1
            for ct in range(4):
                TR(bank(bb)[:, ct * 128:(ct + 1) * 128], zbuf[j][:, ct * 128:(ct + 1) * 128], ident_f,
                   r=[Rz[j], Rc["ident_f"]], w=[RB[bb]])
            for ct in range(4):
                OP("act", lambda e, bb=bb, ct=ct, j=j: e.activation(
                    out=sth[j][:, ct, :], in_=bank(bb)[:, ct * 128:(ct + 1) * 128], func=AF.Tanh,
                    bias=chanP[:, ct, 33:34], scale=chanP[:, ct, 32:33]),
                   r=[Rc["chanP"]], w=[RB[bb], Rsth[j]])
                OP("act", lambda e, bb=bb, ct=ct, j=j: e.activation(
                    out=szz[j][:, ct, :], in_=bank(bb)[:, ct * 128:(ct + 1) * 128], func=AF.Identity,
                    bias=chanP[:, ct, 33:34], scale=chanP[:, ct, 32:33]),
                   r=[Rc["chanP"]], w=[RB[bb], Rszz[j]])
            OP("dve", lambda e, i=i, j=j: e.scalar_tensor_tensor(
                out=ymixT[:, 4:8, i * 128:(i + 1) * 128], in0=sth[j], scalar=1.0, in1=szz[j],
                op0=ALU.add, op1=ALU.mult), r=[Rsth[j], Rszz[j]], w=[Rymc[i]])

        def conv_gen():
            for tb4 in range(4):
                c0 = tb4 * 512
                for ct in range(4):
                    OP("dve", lambda e, ct=ct, c0=c0: e.tensor_scalar(
                        out=cacc[:, ct, :], in0=uT[:, ct, c0:c0 + 512], scalar1=chanP[:, ct, 0:1],
                        scalar2=chanP[:, ct, 31:32], op0=ALU.mult, op1=ALU.add),
                       r=[Rupad, Rc["chanP"]] + [RuT[i] for i in range(max(0, tb4 * 4 - 1), tb4 * 4 + 4)], w=[Rcacc])
                    yield
                    for jt in range(1, 31):
                        OP("dve", lambda e, ct=ct, c0=c0, jt=jt: e.scalar_tensor_tensor(
                            out=cacc[:, ct, :], in0=uT[:, ct, c0 + jt:c0 + jt + 512], scalar=chanP[:, ct, jt:jt + 1],
                            in1=cacc[:, ct, :], op0=ALU.mult, op1=ALU.add), r=[Rcacc], w=[Rcacc])
                        yield
                for k in range(4):
                    yield from ln_tile(tb4 * 4 + k, k * 128)
                    yield
            OP("dve", lambda e: e.memset(cacc[:, :, 0:128], 0.0), w=[Rcacc])
            OP("dve", lambda e: e.tensor_copy(
                out=cacc[:, :, 0:128].rearrange("p c (b j) -> p c b j", b=4)[:, :, :, 0:4], in_=accS),
               r=[RaccS], w=[Rcacc])
            yield from ln_tile(16, 0)
            yield

        cgen = conv_gen()
        cdone = [False]

        def ctick(k=1):
            for _ in range(k):
                if cdone[0]:
                    return
                try:
                    next(cgen)
                except StopIteration:
                    cdone[0] = True

        Rya = [R(f"ya{i}") for i in range(NT)]
        RPT = [R("PT0"), R("PT1"), R("PT2")]
        Ro0 = [R("o0a"), R("o0b")]
        Rt1 = R("t1")
        Ros = R("osum")
        sbi = [0]
        pti = [0]
        hq = [0]
        TICK_EVERY = int(os.environ.get("K_TICK", "2"))
        stepc = [0]
        carry = [None]
        for h in range(4):
            for Q in range(8):
                j2 = hq[0] % 2
                hq[0] += 1
                for c in range(2):
                    ob = 2 + c
                    Ov = bank(ob)[:, 0:258].rearrange("p (q e) -> p q e", q=2)

                    def pv_step(kt, pi, off, Ov=Ov, Q=Q, h=h, ob=ob):
                        for qs in range(2):
                            if kt <= 2 * Q + qs:
                                lc = qs * 128 - off
                                MM(Ov[:, qs, :], PT[pi][:, lc:lc + 128], v_ext[:, kt, h, :],
                                   (kt == 0 and qs == 0), kt == 2 * Q + qs,
                                   r=[RPT[pi], Rv[kt], Rc["ones"]], w=[RB[ob]], skip_group_check=True)

                    def epilogue(c=c, Ov=Ov, j2=j2, Q=Q, h=h, ob=ob):
                        if c == 0:
                            OP("dve", lambda e: e.reciprocal(out=epi[:, 0:2].unsqueeze(2), in_=Ov[:, :, 128:129]),
                               w=[RB[ob], Repi])
                            OP("dve", lambda e: e.tensor_tensor(
                                out=o0[j2], in0=Ov[:, :, 0:128],
                                in1=epi[:, 0:2].unsqueeze(2).to_broadcast([128, 2, 128]),
                                op=ALU.mult), r=[Repi], w=[RB[ob], Ro0[j2]])
                        else:
                            OP("dve", lambda e: e.reciprocal(out=epi[:, 2:4].unsqueeze(2), in_=Ov[:, :, 128:129]),
                               w=[RB[ob], Repi])
                            OP("dve", lambda e: e.tensor_scalar(out=epi[:, 4:6], in0=epi[:, 2:4], scalar1=lam[:, 1:2],
                                                                scalar2=None, op0=ALU.mult),
                               r=[Repi, Rc["lam"]], w=[Repi])
                            OP("dve", lambda e: e.tensor_tensor(
                                out=t1, in0=Ov[:, :, 0:128],
                                in1=epi[:, 4:6].unsqueeze(2).to_broadcast([128, 2, 128]),
                                op=ALU.mult), r=[Repi], w=[RB[ob], Rt1])
                            OP("dve", lambda e: e.tensor_tensor(out=osum, in0=o0[j2], in1=t1, op=ALU.add),
                               r=[Ro0[j2], Rt1], w=[Ros])
                            for qs in range(2):
                                OP("act", lambda e, qs=qs: e.activation(out=junk2[:, 0:128], in_=osum[:, qs, :],
                                                                        func=AF.Square,
                                                                        accum_out=epi[:, 6 + qs:7 + qs]),
                                   r=[Ros], w=[Rjunk, Repi])
                            OP("dve", lambda e: e.tensor_scalar(out=epi[:, 8:10], in0=epi[:, 6:8], scalar1=1.0 / 128,
                                                                scalar2=EPS, op0=ALU.mult, op1=ALU.add),
                               r=[Repi], w=[Repi])
                            OP("pool", lambda e: e.tensor_tensor(out=epi[:, 10:12], in0=epi[:, 8:10],
                                                                 in1=nhalf[:, 0:2], op=ALU.pow),
                               r=[Repi, Rc["nhalf"]], w=[Repi])
                            OP("dve", lambda e: e.tensor_tensor(
                                out=t1, in0=osum, in1=epi[:, 10:12].unsqueeze(2).to_broadcast([128, 2, 128]),
                                op=ALU.mult), r=[Ros, Repi], w=[Rt1])
                            OP("dve", lambda e: e.tensor_tensor(
                                out=y_attn[:, 2 * Q:2 * Q + 2, h * 128:(h + 1) * 128], in0=t1,
                                in1=gsub.unsqueeze(1).to_broadcast([128, 2, 128]), op=ALU.mult),
                               r=[Rt1, Rc["gsub"]], w=[Rya[2 * Q], Rya[2 * Q + 1]])

                    pend = None
                    for kt in range(2 * Q + 2):
                        off = max(0, kt * 128 - Q * 256)
                        ncol = 256 - off
                        sbk = sbi[0] % 2
                        sbi[0] += 1
                        MM(bank(sbk)[:, 0:ncol], kT[c * 64:(c + 1) * 64, h, kt * 128:(kt + 1) * 128],
                           qT[c * 64:(c + 1) * 64, h, Q * 256 + off:(Q + 1) * 256], True, True,
                           r=[RkT[kt], RqT[2 * Q], RqT[2 * Q + 1]], w=[RB[sbk]])
                        pi = pti[0] % 3
                        pti[0] += 1
                        dl = kt - 2 * Q + 14
                        OP("act", lambda e, pi=pi, ncol=ncol, sbk=sbk, h=h, dl=dl: e.activation(
                            out=PT[pi][:, 0:ncol], in_=bank(sbk)[:, 0:ncol], func=AF.Exp,
                            bias=biasT[:, h, dl:dl + 1], scale=0.125),
                           r=[Rc["biasT"]], w=[RB[sbk], RPT[pi]])
                        if kt >= 2 * Q:
                            OP("dve", lambda e, pi=pi: e.tensor_tensor(out=PT[pi][:, 0:128], in0=PT[pi][:, 0:128],
                                                                      in1=tri, op=ALU.mult),
                               r=[RPT[pi], Rc["tri"]], w=[RPT[pi]])
                        if kt == 0 and carry[0] is not None:
                            cpv, cargs, cepi = carry[0]
                            cpv(*cargs)
                            cepi()
                            carry[0] = None
                        if pend is not None:
                            pv_step(*pend)
                        pend = (kt, pi, off)
                        stepc[0] += 1
                        if TICK_EVERY > 0 and stepc[0] % TICK_EVERY == 0:
                            tick()
                        ctick()
                    carry[0] = (pv_step, pend, epilogue)
        if carry[0] is not None:
            cpv, cargs, cepi = carry[0]
            cpv(*cargs)
            cepi()
            carry[0] = None

        for i in range(16):
            b = nextbank()
            tb = bank_bf(b)[:, 0:512].rearrange("p (h n) -> p h n", h=4)
            for h in range(4):
                TR(tb[:, h, :], y_attn[:, i, h * 128:(h + 1) * 128], ident_bf, r=[Rya[i], Rc["ident_bf"]],
                   w=[RB[b]])
            if i % 2 == 0:
                OP("act", lambda e, i=i, tb=tb: e.copy(out=ymixT[:, 0:4, i * 128:(i + 1) * 128], in_=tb),
                   w=[RB[b], Rym[i]])
            else:
                OP("dve", lambda e, i=i, tb=tb: e.tensor_copy(out=ymixT[:, 0:4, i * 128:(i + 1) * 128], in_=tb),
                   w=[RB[b], Rym[i]])
            tick()

        ctick(10 ** 6)
        if stop_after == "attn":
            tick(10 ** 6)
            return finish()

        if stop_after == "mixer":
            tick(10 ** 6)
            return finish()

        tick(10 ** 6)
        S.barrier()
        gD = S.slot(group=True)
        Rg = R("gD", True)
        DMA("sp", gbc[0], g_post_mix.partition_broadcast(128), gD, w=[Rg])
        DMA("sp", gbc[1], g_pre_ffn.partition_broadcast(128), gD, w=[Rg])
        DMA("sp", gbc[2], g_post_ffn.partition_broadcast(128), gD, w=[Rg])

        RwD = [R(f"wD{i}") for i in range(4)]
        wDsl = [S.slot() for _ in range(4)]
        wq_ = []
        wi = [0]
        sc_out_v = sc_out.rearrange("(c p) n -> p c n", p=128)
        sc_ff1_v = sc_ff1.rearrange("(c p) n -> p c n", p=128)
        sc_ff2_v = sc_ff2.rearrange("(c p) n -> p c n", p=128)
        sc_gate_v = sc_gate.rearrange("(c p) n -> p c n", p=128)
        sc_ple_v = sc_ple.rearrange("(c p) n -> p c n", p=128)

        def chunk_list():
            L = []
            for half in range(2):
                L.append(("out", sc_out_v[:, :, half * 512:(half + 1) * 512], Rsc["out"], (8, 512)))
            for m in range(8):
                L.append(("ff1", sc_ff1_v[:, :, m * 512:(m + 1) * 512], Rsc["ff1"], (8, 512)))
            for half in range(2):
                for m in range(4):
                    L.append(("ff2", sc_ff2_v[:, m * 8:(m + 1) * 8, half * 512:(half + 1) * 512], Rsc["ff2"], (8, 512)))
            for half in range(2):
                L.append(("gate", sc_gate_v[:, :, half * 512:(half + 1) * 512], Rsc["gate"], (8, 512)))
            L.append(("ple", sc_ple_v, Rsc["ple"], (2, 1024)))
            return L

        pending = []
        issued = []

        def issue_one():
            if not pending:
                return
            kind, src, rsc, (a, n) = pending.pop(0)
            bi = wi[0] % 4
            wi[0] += 1
            view = wsD[bi][:, 0:a * n].rearrange("p (c n) -> p c n", c=a)
            DMA("sp", view, src, wDsl[bi], r=[rsc], w=[RwD[bi]])
            issued.append((bi, view))

        def get_chunk():
            bi, view = issued.pop(0)
            return view, RwD[bi]

        Rh = [R(f"h{i}") for i in range(4)]
        RhnT = [R(f"hnT{i}") for i in range(5)]
        Rf1 = R("f1T")
        Rfb = [R(f"fb{i}") for i in range(4)]
        Rxr = [R("xr0"), R("xr1")]
        xrsl = [S.slot(), S.slot()]
        Rpr = [R("pr0"), R("pr1")]
        prsl = [S.slot(), S.slot()]
        Rpb = [R("pb0"), R("pb1")]
        RpeT = [R("peT0"), R("peT1")]
        Rtmp = [R("tmpD0"), R("tmpD1"), R("tmpD2")]
        ysl = S.slot()
        Rhbf = [R("hbf0"), R("hbf1")]
        Rj3 = R("junk3")
        statD = small[:, 484:484 + 24]
        RstD = [R(f"stD{i}") for i in range(4)]

        def rstd_D(src_list, n_total, si):
            c = statD[:, si * 6:si * 6 + 6]
            rs = RstD[si]
            for k, (ap, res, bk) in enumerate(src_list):
                n = ap.shape[-1]
                if bk is None:
                    OP("act", lambda e, ap=ap, n=n, k=k: e.activation(out=junk3[:, 0:n], in_=ap, func=AF.Square,
                                                                      accum_out=c[:, k:k + 1]),
                       r=[res], w=[Rj3, rs])
                else:
                    OP("act", lambda e, ap=ap, n=n, k=k: e.activation(out=junk3[:, 0:n], in_=ap, func=AF.Square,
                                                                      accum_out=c[:, k:k + 1]),
                       r=[], w=[Rj3, rs] + [RB[x] for x in bk])
            if len(src_list) == 2:
                OP("dve", lambda e: e.tensor_tensor(out=c[:, 0:1], in0=c[:, 0:1], in1=c[:, 1:2], op=ALU.add),
                   r=[rs], w=[rs])
            OP("dve", lambda e: e.tensor_scalar(out=c[:, 2:3], in0=c[:, 0:1], scalar1=1.0 / n_total, scalar2=EPS,
                                                op0=ALU.mult, op1=ALU.add), r=[rs], w=[rs])
            OP("pool", lambda e: e.tensor_tensor(out=c[:, 3:4], in0=c[:, 2:3], in1=nhalf[:, 0:1], op=ALU.pow),
               r=[rs, Rc["nhalf"]], w=[rs])
            return c[:, 3:4], rs

        def load_rows(dst, dres, slot, src_p, src_s, i, width):
            if i < 16:
                DMA("sp", dst[:, 0:width], src_p[i * 128:(i + 1) * 128, :], slot, w=[dres])
            else:
                OP("pool", lambda e: e.memset(dst[:, 0:width], 0.0), w=[dres])
                for bl in range(4):
                    DMA("sp", dst[bl * 32:bl * 32 + 4, 0:width], src_s[bl * 4:bl * 4 + 4, :], slot, w=[dres])

        def to_T(src_bf, sres, dstT, dres, col, nchunk, pbk):
            tb = bank_bf(pbk)[:, 0:nchunk * 128].rearrange("p (c n) -> p c n", c=nchunk)
            for c in range(nchunk):
                TR(tb[:, c, :], src_bf[:, c * 128:(c + 1) * 128], ident_bf, r=[sres, Rc["ident_bf"]], w=[RB[pbk]])
            OP("act", lambda e: e.copy(out=dstT[:, 0:nchunk, col:col + 128], in_=tb), w=[RB[pbk], dres])

        BLOCKS = [[0, 1, 2, 3], [4, 5, 6, 7], [8, 9, 10, 11], [12, 13, 14, 15], [16]]
        if os.environ.get('K_BLKS'):
            BLOCKS = [BLOCKS[int(c)] for c in os.environ['K_BLKS'].split(',')]
        tcount = [0]
        for blk in BLOCKS:
            nt = len(blk)
            ncols = nt * 128
            pending.extend(chunk_list())
            while len(issued) < 3 and pending:
                issue_one()
            wo = [get_chunk(), get_chunk()]
            prevT = None
            for ti, i in enumerate(blk):
                j = tcount[0] % 2
                tcount[0] += 1
                pb2 = (0, 1) if j == 0 else (2, 3)
                mps = bank(pb2[0], 2)
                for half in range(2):
                    wv, wr = wo[half]
                    for c in range(8):
                        MM(bank(pb2[half]), ymixT[:, c, i * 128:(i + 1) * 128], wv[:, c, :], c == 0, c == 7,
                           r=[wr, Rym[i], Rymc[i]], w=[RB[pb2[half]]])
                if prevT is not None:
                    to_T(*prevT)
                load_rows(xrD[j], Rxr[j], xrsl[j], x_p, x_s, i, D)
                rstd, rs = rstd_D([(mps, None, pb2)], D, 0)
                OP("dve", lambda e, mps=mps, rstd=rstd: e.scalar_tensor_tensor(
                    out=tmpD[0], in0=mps, scalar=rstd, in1=gbc[0], op0=ALU.mult, op1=ALU.mult),
                   r=[rs, Rg], w=[RB[pb2[0]], RB[pb2[1]], Rtmp[0]])
                OP("dve", lambda e, ti=ti, j=j: e.tensor_tensor(out=hblk[:, ti, :], in0=tmpD[0], in1=xrD[j], op=ALU.add),
                   r=[Rtmp[0], Rxr[j]], w=[Rh[ti]])
                rstd2, rs2 = rstd_D([(hblk[:, ti, :], Rh[ti], None)], D, 1)
                OP("dve", lambda e, ti=ti, j=j, rstd2=rstd2: e.scalar_tensor_tensor(
                    out=hbfD[j], in0=hblk[:, ti, :], scalar=rstd2, in1=gbc[1], op0=ALU.mult, op1=ALU.mult),
                   r=[Rh[ti], rs2, Rg], w=[Rhbf[j]])
                prevT = (hbfD[j], Rhbf[j], hnT, RhnT[ti], ti * 128, 8, 4 + j)
            to_T(*prevT)
            if stop_after == "d1":
                S.barrier()
                dsl2 = S.slot()
                S.dma("sp", lambda e: e.dma_start(out=dbg_f, in_=hblk.rearrange("p i n -> p (i n)")), dsl2, is_output=True)
                return finish()
            issue_one()
            issue_one()
            rot = [0]
            for m in range(8):
                wv, wr = get_chunk()
                for f in range(4):
                    fc = 4 * m + f
                    bk = rot[0] % 4
                    rot[0] += 1
                    for c in range(8):
                        MM(bank(bk)[:, 0:ncols], wv[:, c, f * 128:(f + 1) * 128], hnT[:, c, 0:ncols], c == 0, c == 7,
                           r=[wr] + RhnT[0:nt], w=[RB[bk]])
                    sq = tmpD[1 + fc % 2]
                    rsq = Rtmp[1 + fc % 2]
                    OP("act", lambda e, bk=bk, sq=sq, ncols=ncols: e.activation(
                        out=sq[:, 0:ncols], in_=bank(bk)[:, 0:ncols], func=AF.Square), w=[RB[bk], rsq])
                    OP("dve", lambda e, bk=bk, sq=sq, fc=fc, ncols=ncols: e.scalar_tensor_tensor(
                        out=f1T[:, fc, 0:ncols], in0=bank(bk)[:, 0:ncols], scalar=0.0, in1=sq[:, 0:ncols],
                        op0=ALU.is_gt, op1=ALU.mult), r=[rsq], w=[RB[bk], Rf1])
                issue_one()
            for half in range(2):
                for m in range(4):
                    wv, wr = get_chunk()
                    for ti in range(nt):
                        for f in range(8):
                            fc = 8 * m + f
                            MM(bank(ti), f1T[:, fc, ti * 128:(ti + 1) * 128], wv[:, f, :],
                               (m == 0 and f == 0), (m == 3 and f == 7), r=[wr, Rf1], w=[RB[ti]])
                    issue_one()
                if half == 0:
                    for ti in range(nt):
                        OP("act", lambda e, ti=ti: e.copy(out=fblk[:, ti, :], in_=bank(ti)), w=[RB[ti], Rfb[ti]])
            for ti, i in enumerate(blk):
                rstd3, rs3 = rstd_D([(fblk[:, ti, :], Rfb[ti], None), (bank(ti), None, (ti,))], D, 2)
                OP("dve", lambda e, ti=ti, rstd3=rstd3: e.scalar_tensor_tensor(
                    out=tmpD[0][:, 0:512], in0=fblk[:, ti, :], scalar=rstd3, in1=gbc[2][:, 0:512],
                    op0=ALU.mult, op1=ALU.mult), r=[Rfb[ti], rs3, Rg], w=[Rtmp[0]])
                OP("dve", lambda e, ti=ti, rstd3=rstd3: e.scalar_tensor_tensor(
                    out=tmpD[0][:, 512:1024], in0=bank(ti), scalar=rstd3, in1=gbc[2][:, 512:1024],
                    op0=ALU.mult, op1=ALU.mult), r=[rs3, Rg], w=[RB[ti], Rtmp[0]])
                OP("dve", lambda e, ti=ti: e.tensor_tensor(out=hblk[:, ti, :], in0=hblk[:, ti, :], in1=tmpD[0],
                                                          op=ALU.add), r=[Rtmp[0], Rh[ti]], w=[Rh[ti]])
            if stop_after == "d3":
                S.barrier()
                dsl2 = S.slot()
                S.dma("sp", lambda e: e.dma_start(out=dbg_f, in_=hblk.rearrange("p i n -> p (i n)")), dsl2, is_output=True)
                return finish()
            wg2 = [get_chunk(), get_chunk()]
            wpv, wpr = get_chunk()
            jb = tcount[0]
            tcount[0] += nt

            def d4_front(ti, i):
                j = (jb + ti) % 2
                OP("act", lambda e, ti=ti, j=j: e.copy(out=hbfD[j], in_=hblk[:, ti, :]), r=[Rh[ti]], w=[Rhbf[j]])
                to_T(hbfD[j], Rhbf[j], hnT, RhnT[ti], ti * 128, 8, 4 + j)
                load_rows(prD[j], Rpr[j], prsl[j], p_p, p_s, i, 256)
                OP("dve", lambda e, j=j: e.tensor_copy(out=pbD[j], in_=prD[j]), r=[Rpr[j]], w=[Rpb[j]])
                to_T(pbD[j], Rpb[j], peT[j], RpeT[j], 0, 2, 6 + j)

            d4_front(0, blk[0])
            for ti, i in enumerate(blk):
                j = (jb + ti) % 2
                if ti + 1 < nt:
                    d4_front(ti + 1, blk[ti + 1])
                for half in range(2):
                    wv, wr = wg2[half]
                    for c in range(8):
                        MM(bank(half), hnT[:, c, ti * 128:(ti + 1) * 128], wv[:, c, :], c == 0, c == 7,
                           r=[wr, RhnT[ti]], w=[RB[half]])
                for half in range(2):
                    for c in range(2):
                        MM(bank(2 + half), peT[j][:, c, :], wpv[:, c, half * 512:(half + 1) * 512], c == 0, c == 1,
                           r=[wpr, RpeT[j]], w=[RB[2 + half]])
                OP("act", lambda e: e.activation(out=tmpD[1], in_=bank(0, 2), func=AF.Tanh, scale=0.5),
                   w=[RB[0], RB[1], Rtmp[1]])
                OP("dve", lambda e: e.scalar_tensor_tensor(out=tmpD[2], in0=tmpD[1], scalar=1.0, in1=bank(2, 2),
                                                           op0=ALU.add, op1=ALU.mult),
                   r=[Rtmp[1]], w=[RB[2], RB[3], Rtmp[2]])
                OP("dve", lambda e, ti=ti: e.scalar_tensor_tensor(out=tmpD[2], in0=tmpD[2], scalar=0.5,
                                                                 in1=hblk[:, ti, :], op0=ALU.mult, op1=ALU.add),
                   r=[Rtmp[2], Rh[ti]], w=[Rtmp[2]])
                if i < 16:
                    DMA("sp", y_p[i * 128:(i + 1) * 128, :], tmpD[2], ysl, r=[Rtmp[2]], is_output=True)
                else:
                    for bl in range(4):
                        DMA("sp", y_s[bl * 4:bl * 4 + 4, :], tmpD[2][bl * 32:bl * 32 + 4, :], ysl, r=[Rtmp[2]],
                            is_output=True)
            assert not issued and not pending, (len(issued), len(pending))
            if os.environ.get("K_BLKBAR", "0") == "1":
                S.barrier()
            if stop_after == "d4":
                S.barrier()
                dsl2 = S.slot()
                S.dma("sp", lambda e: e.dma_start(out=dbg_f[:, 0:1024], in_=tmpD[1]), dsl2, is_output=True)
                S.dma("sp", lambda e: e.dma_start(out=dbg_f[:, 1024:2048], in_=tmpD[2]), dsl2, is_output=True)
                S.dma("sp", lambda e: e.dma_start(out=dbg_f[:, 2048:3072], in_=hblk[:, 3, :]), dsl2, is_output=True)
                return finish()

        return finish()


_CACHE = {}


def _in_maps(inputs, cores):
    f = lambda a: np.ascontiguousarray(np.asarray(a))
    ck = f(inputs["cache_k"]).reshape(-1, 512)[:NPOOL * 128]
    cv = f(inputs["cache_v"]).reshape(-1, 512)[:NPOOL * 128]
    shared = dict(
        cache_k=ck, cache_v=cv,
        w_in=f(inputs["w_in"][0]), w_out=f(inputs["w_out"][0]),
        lamq1=f(inputs["lambda_q1"]), lamk1=f(inputs["lambda_k1"]),
        lamq2=f(inputs["lambda_q2"]), lamk2=f(inputs["lambda_k2"]),
        g_subln=f(inputs["g_subln"]), w_dw=f(inputs["w_dw"][0]), b_dw=f(inputs["b_dw"]),
        ln_g=f(inputs["ln_conv_g"]), ln_b=f(inputs["ln_conv_b"]),
        g_pre_mix=f(inputs["g_pre_mix"]), g_post_mix=f(inputs["g_post_mix"]),
        g_pre_ffn=f(inputs["g_pre_ffn"]), g_post_ffn=f(inputs["g_post_ffn"]),
        w_ff1=f(inputs["w_ff1"][0]), w_ff2=f(inputs["w_ff2"][0]),
        w_ple=f(inputs["w_ple"][0]), w_gate=f(inputs["w_ple_gate"][0]),
    )
    in_maps = []
    for c in cores:
        m = dict(shared)
        m["x_p"] = f(inputs["x_prompt"][c])
        m["x_s"] = f(inputs["x_sample"][4 * c:4 * c + 4]).reshape(16, D)
        m["st_conv"] = f(inputs["state_conv"][0, 4 * c:4 * c + 4]).reshape(120, 512)
        m["ptab"] = f(inputs["page_table"][4 * c:4 * c + 4]).reshape(1, 256).astype(np.int32)
        m["p_p"] = f(inputs["p_prompt"][0, c])
        m["p_s"] = f(inputs["p_sample"][0, 4 * c:4 * c + 4]).reshape(16, 256)
        in_maps.append(m)
    return in_maps


def kernel(**inputs):
    n = 8
    if "nc" not in _CACHE:
        _CACHE["nc"] = build_program()
    nc = _CACHE["nc"]
    in_maps = _in_maps(inputs, list(range(n)))
    res = run_bass_kernel_spmd(nc, in_maps, core_ids=list(range(n)))
    rs = res.results
    y_prompt = np.stack([rs[c]["y_p"] for c in range(n)])
    y_sample = np.concatenate([rs[c]["y_s"].reshape(4, 4, D) for c in range(n)])
    k_prompt = np.stack([rs[c]["k_p"].reshape(T, 4, 2, 64) for c in range(n)])[None]
    v_prompt = np.stack([rs[c]["v_p"].reshape(T, 4, 128) for c in range(n)])[None]
    conv_prompt = np.stack([rs[c]["c_p"] for c in range(n)])[None]
    k_sample = np.concatenate([rs[c]["k_s"].reshape(4, 4, 4, 2, 64) for c in range(n)])[None]
    v_sample = np.concatenate([rs[c]["v_s"].reshape(4, 4, 4, 128) for c in range(n)])[None]
    conv_sample = np.concatenate([rs[c]["c_s"].reshape(4, 30, 512) for c in range(n)])[None]
    return (y_prompt, y_sample, k_prompt, v_prompt, conv_prompt, k_sample, v_sample, conv_sample)
```

```python
import contextlib
import numpy as np
import concourse.bass as bass
import concourse.mybir as mybir
from concourse.bass_utils import run_bass_kernel_spmd

F32 = mybir.dt.float32
BF16 = mybir.dt.bfloat16
I32 = mybir.dt.int32
AF = mybir.ActivationFunctionType
ALU = mybir.AluOpType
AX = mybir.AxisListType

D = 1024
T = 2048
NT = 17
NC_ = 2176
NPG = 64
import os
NPOOL = int(os.environ.get('K_NPOOL', '2560'))
DFF = 4096
EPS = 1e-6
LAM_INIT = 0.2
SLOPES = [2.0 ** (-2.0 * (h + 1)) for h in range(4)]


class Ev:
    __slots__ = ("sem", "val", "eng", "slot")

    def __init__(self, sem, val, eng, slot=None):
        self.sem, self.val, self.eng, self.slot = sem, val, eng, slot

    def value(self):
        return self.slot.count if self.slot is not None else self.val


class Res:
    __slots__ = ("name", "w", "r", "const")

    def __init__(self, name="", const=False):
        self.name, self.w, self.r, self.const = name, None, {}, const


class Slot:
    def __init__(self, sem, group=False):
        self.sem, self.count, self.group = sem, 0, group


class Sched:
    ENG = ("pe", "act", "dve", "pool", "sp")

    def __init__(self, nc, sems):
        self.nc = nc
        self._sems = list(sems)
        self.streams = {e: [] for e in self.ENG}
        self.esem = {e: self._sems.pop() for e in ("pe", "act", "dve", "pool")}
        self.cnt = {e: 0 for e in self.ENG}
        self.waited = {e: {} for e in self.ENG}
        self.out_events = []
        self.slots = []
        self._excl = None

    def slot(self, group=False):
        s = Slot(self._sems.pop(), group)
        self.slots.append(s)
        return s

    def _need(self, eng, ev, waits, raw):
        if ev is None or (ev.slot is not None and ev.slot is self._excl):
            return
        if ev.eng == eng and (not raw or eng in ("pe", "sp")):
            return
        key = id(ev.sem)
        if ev.slot is not None:
            if self.waited[eng].get((key, "f")):
                return
            self.waited[eng][(key, "f")] = True
            waits.append(ev)
            return
        if ev.val <= self.waited[eng].get(key, -1):
            return
        self.waited[eng][key] = ev.val
        waits.append(ev)

    def _deps(self, eng, reads, writes):
        waits = []
        for r in reads:
            self._need(eng, r.w, waits, True)
        for w in writes:
            self._need(eng, w.w, waits, False)
            for ev in w.r.values():
                self._need(eng, ev, waits, False)
        return waits

    def _record(self, ev, key, reads, writes):
        for r in reads:
            if not r.const:
                r.r[key] = ev
        for w in writes:
            w.w = ev
            w.r = {}

    def op(self, eng, fn, reads=(), writes=()):
        waits = self._deps(eng, reads, writes)
        self.cnt[eng] += 1
        ev = Ev(self.esem[eng], self.cnt[eng], eng)
        self.streams[eng].append((waits, fn, (self.esem[eng], 1)))
        self._record(ev, eng, reads, writes)
        return ev

    def dma(self, q, fn, slot, reads=(), writes=(), is_output=False):
        self._excl = slot if slot.group else None
        waits = self._deps(q, reads, writes)
        self._excl = None
        slot.count += 16
        ev = Ev(slot.sem, None if slot.group else slot.count, "dma", slot if slot.group else None)
        self.streams[q].append((waits, fn, (slot.sem, 16)))
        self._record(ev, ("dma", id(slot.sem)), reads, writes)
        if is_output:
            self.out_events.append(ev)
        return ev

    def barrier(self, exclude=()):
        evs = [Ev(self.esem[e], self.cnt[e], e) for e in ("pe", "act", "dve", "pool") if self.cnt[e] > 0]
        devs = [Ev(s.sem, s.count, "dma") for s in self.slots if s.count > 0 and s not in exclude]
        for eng in self.ENG:
            waits = []
            for ev in evs + devs:
                if ev.eng == eng:
                    continue
                self._need(eng, ev, waits, False)
            if waits:
                self.streams[eng].append((waits, None, None))

    def emit(self):
        nc = self.nc
        fin = {}
        for ev in self.out_events:
            k, v = id(ev.sem), ev.value()
            if k not in fin or fin[k][1] < v:
                fin[k] = (ev.sem, v)
        streams = self.streams

        def replay(name, e):
            for waits, fn, inc in streams[name]:
                for ev in waits:
                    e.wait_ge(ev.sem, ev.value())
                if fn is None:
                    continue
                ins = fn(e)
                if inc is not None:
                    ins.then_inc(inc[0], inc[1])
            if name == "sp":
                for sem, v in fin.values():
                    e.wait_ge(sem, v)

        with nc.Block() as block:
            @block.tensor
            def _(e):
                replay("pe", e)

            @block.scalar
            def _(e):
                replay("act", e)

            @block.vector
            def _(e):
                replay("dve", e)

            @block.gpsimd
            def _(e):
                replay("pool", e)

            @block.sync
            def _(e):
                replay("sp", e)


def build_program(stop_after="all"):
    nc = bass.Bass("TRN2", target_bir_lowering=False)

    def din(name, shape, dt=F32):
        return nc.dram_tensor(name, list(shape), dt, kind="ExternalInput").ap()

    def dout(name, shape, dt=F32):
        return nc.dram_tensor(name, list(shape), dt, kind="ExternalOutput").ap()

    x_p = din("x_p", [T, D])
    x_s = din("x_s", [16, D])
    cache_k = din("cache_k", [NPOOL * 128, 512])
    cache_v = din("cache_v", [NPOOL * 128, 512])
    st_conv = din("st_conv", [4 * 30, 512])
    ptab = din("ptab", [1, 4 * NPG], I32)
    p_p = din("p_p", [T, 256])
    p_s = din("p_s", [16, 256])
    w_in = din("w_in", [D, 2560])
    w_out = din("w_out", [D, D])
    lamq1 = din("lamq1", [1, 64])
    lamk1 = din("lamk1", [1, 64])
    lamq2 = din("lamq2", [1, 64])
    lamk2 = din("lamk2", [1, 64])
    g_subln = din("g_subln", [1, 128])
    w_dw = din("w_dw", [31, 512])
    b_dw = din("b_dw", [1, 512])
    ln_g = din("ln_g", [1, 512])
    ln_b = din("ln_b", [1, 512])
    g_pre_mix = din("g_pre_mix", [1, D])
    g_post_mix = din("g_post_mix", [1, D])
    g_pre_ffn = din("g_pre_ffn", [1, D])
    g_post_ffn = din("g_post_ffn", [1, D])
    w_ff1 = din("w_ff1", [D, DFF])
    w_ff2 = din("w_ff2", [DFF, D])
    w_ple = din("w_ple", [256, D])
    w_gate = din("w_gate", [D, D])
    y_p = dout("y_p", [T, D])
    y_s = dout("y_s", [16, D])
    k_p = dout("k_p", [T, 512])
    v_p = dout("v_p", [T, 512])
    c_p = dout("c_p", [30, 512])
    k_s = dout("k_s", [16, 512])
    v_s = dout("v_s", [16, 512])
    c_s = dout("c_s", [4 * 30, 512])
    dbg_ym = dout("dbg_ym", [128, 8 * NC_], BF16) if stop_after != "all" else None
    dbg_f = dout("dbg_f", [128, 4096]) if stop_after != "all" else None
    sc_out = nc.dram_tensor("sc_out", [D, D], BF16).ap()
    sc_ff1 = nc.dram_tensor("sc_ff1", [D, DFF], BF16).ap()
    sc_ff2 = nc.dram_tensor("sc_ff2", [DFF, D], BF16).ap()
    sc_gate = nc.dram_tensor("sc_gate", [D, D], BF16).ap()
    sc_ple = nc.dram_tensor("sc_ple", [256, D], BF16).ap()

    es = contextlib.ExitStack()
    with es:
        sems = [es.enter_context(nc.semaphore(f"s{i}")) for i in range(96)]
        S = Sched(nc, sems)
        big = es.enter_context(nc.sbuf_tensor("big", [128, 212800 // 4], F32))
        pp = es.enter_context(nc.psum_tensor("pp", [128, 4096], F32))

        class Arena:
            def __init__(self, base, limit):
                self.p, self.limit = base, limit

            def take(self, nbytes, dt, pattern=None, **kw):
                assert nbytes % 4 == 0
                off = self.p
                self.p += (nbytes + 31) // 32 * 32
                assert self.p <= self.limit, (self.p, self.limit)
                v = big[:, off // 4:(off + nbytes) // 4]
                if dt != F32:
                    v = v.bitcast(dt)
                if pattern:
                    v = v.rearrange(pattern, **kw)
                return v

        TOTAL = 212800
        A = Arena(0, TOTAL)
        ident_bf = A.take(256, BF16)
        ident_f = A.take(512, F32)
        tri = A.take(256, BF16)
        nhalf = A.take(32, F32)
        biasT = A.take(4 * 16 * 4, F32, "p (h d) -> p h d", h=4)
        lam = A.take(32, F32)
        gsub = A.take(512, F32)
        chanP = A.take(4 * 34 * 4, F32, "p (c j) -> p c j", c=4)
        small = A.take(2048, F32)
        NS = 4
        kpg = [A.take(1024, BF16) for _ in range(NS)]
        vpg = [A.take(1024, BF16) for _ in range(NS)]
        kTp = [A.take(1024, BF16, "p (h t) -> p h t", h=4) for _ in range(2)]
        PTs = [A.take(64, BF16) for _ in range(2)]
        qblk = A.take(4 * 4 * 8 * 2, BF16, "p (b h j) -> p b h j", b=4, h=4)
        us_ext = A.take(4 * 4 * 34 * 4, F32, "p (c b j) -> p c b j", c=4, b=4)
        idx_i = A.take(256 * 4, I32)
        pt_i = A.take(256 * 4, I32)
        pt_f = A.take(256 * 4, F32)
        idx_f = pt_f
        iop_f = A.take(32, F32)
        Atab = A.take(3 * 128 * 2, BF16, "p (r t) -> p r t", r=3)
        Btab = A.take(65 * 32 * 2, BF16, "p (j c) -> p j c", j=65)
        smask = A.take(128 * 2, BF16)
        Osb = A.take(4 * 129 * 4, F32, "p (h e) -> p h e", h=4)
        Dmat = A.take(32, F32)
        s_tmp = A.take(5504, F32)
        XY = A.take(8 * NC_ * 2, BF16, "p (c n) -> p c n", c=8)
        pers_end = A.p
        qT = A.take(4 * NC_ * 2, BF16, "p (h n) -> p h n", h=4)
        kT = A.take(4 * NC_ * 2, BF16, "p (h n) -> p h n", h=4)
        v_ext = A.take(NT * 4 * 129 * 2, BF16, "p (i h e) -> p i h e", i=NT, h=4)
        UW = 30 + NC_
        uT = A.take(4 * UW * 4, F32, "p (c n) -> p c n", c=4)
        qkvu_end = A.p
        r1_base = A.p
        xs = A.take(4096, F32)
        xst = [A.take(4096, F32) for _ in range(2)]
        xn = [A.take(2048, BF16) for _ in range(2)]
        wst = [A.take(8192, BF16, "p (c n) -> p c n", c=8) for _ in range(2)]
        kvf = [A.take(2048, F32) for _ in range(3)]
        kbf = [A.take(1024, BF16) for _ in range(2)]
        gpre = A.take(4096, F32)
        gth = [A.take(2048, F32) for _ in range(2)]
        gah = [A.take(2048, F32) for _ in range(2)]
        junk = A.take(2048, BF16)
        m1_end = A.p
        A2 = Arena(r1_base, TOTAL)
        y_attn = A2.take(NT * 512 * 2, BF16, "p (i n) -> p i n", i=NT)
        cacc = A2.take(4 * 512 * 4, F32, "p (c n) -> p c n", c=4)
        zbuf = [A2.take(2048, F32) for _ in range(2)]
        PT = [A2.take(512, BF16) for _ in range(3)]
        o0 = [A2.take(1024, F32, "p (q e) -> p q e", q=2) for _ in range(2)]
        t1 = A2.take(1024, F32, "p (q e) -> p q e", q=2)
        osum = A2.take(1024, F32, "p (q e) -> p q e", q=2)
        osq = A2.take(1024, F32, "p (q e) -> p q e", q=2)
        sth = [A2.take(2048, F32, "p (c n) -> p c n", c=4) for _ in range(2)]
        szz = [A2.take(2048, F32, "p (c n) -> p c n", c=4) for _ in range(2)]
        sprod = A2.take(4 * 4 * 31 * 4, F32, "p (b t j) -> p b t j", b=4, t=4)
        accS = A2.take(4 * 4 * 4 * 4, F32, "p (c b t) -> p c b t", c=4, b=4)
        ststage = A2.take(2048, F32)
        cstage = A2.take(2048, F32)
        ustm = A2.take(2048, F32)
        junk2 = A2.take(2048, BF16)
        A3 = Arena(pers_end, TOTAL)
        hblk = A3.take(4 * 4096, F32, "p (i n) -> p i n", i=4)
        hnT = A3.take(8 * 512 * 2, BF16, "p (c n) -> p c n", c=8)
        f1T = A3.take(32 * 512 * 2, BF16, "p (c n) -> p c n", c=32)
        wsD = [A3.take(8192, BF16) for _ in range(4)]
        xrD = [A3.take(4096, F32) for _ in range(2)]
        prD = [A3.take(1024, F32) for _ in range(2)]
        pbD = [A3.take(512, BF16) for _ in range(2)]
        peT = [A3.take(512, BF16, "p (c n) -> p c n", c=2) for _ in range(2)]
        tmpD = [A3.take(4096, F32) for _ in range(3)]
        hbfD = [A3.take(2048, BF16) for _ in range(2)]
        gbc = [A3.take(4096, F32) for _ in range(3)]
        junk3 = A3.take(2048, BF16)
        fblk = A3.take(4 * 2048, F32, "p (i n) -> p i n", i=4)

        def bank(b, n=1):
            return pp[:, b * 512:(b + n) * 512]

        RB = [Res(f"bank{b}") for b in range(8)]

        def bank_bf(b):
            return bank(b).bitcast(BF16)

        def R(name="", const=False):
            return Res(name, const)

        def OP(eng, fn, r=(), w=()):
            return S.op(eng, fn, r, w)

        def MM(out, lhsT, rhs, start, stop, r, w, **kw):
            return S.op("pe", lambda e: e.matmul(out, lhsT=lhsT, rhs=rhs, start=start, stop=stop, **kw), r, w)

        def TR(out, in_, ident, r, w):
            return S.op("pe", lambda e: e.transpose(out=out, in_=in_, identity=ident), r, w)

        def DMA(q, out, in_, slot, r=(), w=(), is_output=False, **kw):
            return S.dma(q, lambda e: e.dma_start(out=out, in_=in_, **kw), slot, r, w, is_output)

        so = S.slot(group=True)

        Rc = {n: R(n, True) for n in ["ident_bf", "ident_f", "tri", "nhalf", "biasT", "lam", "gsub", "chanP",
                                      "iop", "Atab", "Btab", "smask", "Dmat", "gpre", "ones"]}
        OP("pool", lambda e: e.memset(ident_bf, 0.0), w=[Rc["ident_bf"]])
        OP("pool", lambda e: e.affine_select(out=ident_bf, in_=ident_bf, compare_op=ALU.not_equal, fill=1.0,
                                             base=0, pattern=[[-1, 128]], channel_multiplier=1),
           r=[Rc["ident_bf"]], w=[Rc["ident_bf"]])
        OP("pool", lambda e: e.memset(ident_f, 0.0), w=[Rc["ident_f"]])
        OP("pool", lambda e: e.affine_select(out=ident_f, in_=ident_f, compare_op=ALU.not_equal, fill=1.0,
                                             base=0, pattern=[[-1, 128]], channel_multiplier=1),
           r=[Rc["ident_f"]], w=[Rc["ident_f"]])
        OP("pool", lambda e: e.memset(tri, 1.0), w=[Rc["tri"]])
        OP("pool", lambda e: e.affine_select(out=tri, in_=tri, compare_op=ALU.is_ge, fill=0.0,
                                             base=0, pattern=[[1, 128]], channel_multiplier=-1),
           r=[Rc["tri"]], w=[Rc["tri"]])
        OP("pool", lambda e: e.memset(nhalf, -0.5), w=[Rc["nhalf"]])
        iot = small[:, 0:16]
        Riot = R("iot")
        OP("pool", lambda e: e.iota(iot, pattern=[[128, 16]], base=-128 * 15, channel_multiplier=1,
                                    allow_small_or_imprecise_dtypes=True), w=[Riot])
        for h in range(4):
            OP("pool", lambda e, h=h: e.tensor_scalar(out=biasT[:, h, :], in0=iot, scalar1=SLOPES[h], scalar2=None,
                                                      op0=ALU.mult), r=[Riot], w=[Rc["biasT"]])
        OP("pool", lambda e: e.iota(iop_f, pattern=[[0, 8]], base=0, channel_multiplier=1,
                                    allow_small_or_imprecise_dtypes=True), w=[Rc["iop"]])

        if stop_after == "p0a":
            S.barrier()
            S.emit()
            return nc
        pslot = S.slot(group=True)
        lamv = small[:, 16:16 + 256].rearrange("p (a d) -> p a d", a=4)
        Rlamv = R("lamv")
        for i, ap in enumerate([lamq1, lamk1, lamq2, lamk2]):
            DMA("sp", lamv[:, i, :], ap.partition_broadcast(128), pslot, w=[Rlamv])
        Rgs = R("gsraw")
        DMA("sp", gsub, g_subln.partition_broadcast(128), pslot, w=[Rgs])
        DMA("sp", gpre, g_pre_mix.partition_broadcast(128), pslot, w=[Rc["gpre"]])
        ptm = xst[1]
        Rptm = R("ptm")
        DMA("sp", ptm[0:31, 0:512], w_dw, pslot, w=[Rptm])
        DMA("sp", ptm[31:32, 0:512], b_dw, pslot, w=[Rptm])
        DMA("sp", ptm[32:33, 0:512], ln_g, pslot, w=[Rptm])
        DMA("sp", ptm[33:34, 0:512], ln_b, pslot, w=[Rptm])
        if stop_after == "p0b":
            S.barrier()
            S.emit()
            return nc
        lprod = small[:, 16 + 256:16 + 256 + 128].rearrange("p (a d) -> p a d", a=2)
        Rlp = R("lprod")
        OP("dve", lambda e: e.tensor_tensor(out=lprod[:, 0, :], in0=lamv[:, 0, :], in1=lamv[:, 1, :], op=ALU.mult),
           r=[Rlamv], w=[Rlp])
        OP("dve", lambda e: e.tensor_tensor(out=lprod[:, 1, :], in0=lamv[:, 2, :], in1=lamv[:, 3, :], op=ALU.mult),
           r=[Rlamv], w=[Rlp])
        lsum = small[:, 400:402]
        Rls = R("lsum")
        OP("dve", lambda e: e.tensor_reduce(out=lsum, in_=lprod, axis=AX.X, op=ALU.add), r=[Rlp], w=[Rls])
        lexp = small[:, 402:404]
        Rle = R("lexp")
        OP("act", lambda e: e.activation(out=lexp, in_=lsum, func=AF.Exp), r=[Rls], w=[Rle])
        OP("dve", lambda e: e.tensor_tensor(out=lam[:, 0:1], in0=lexp[:, 0:1], in1=lexp[:, 1:2], op=ALU.subtract),
           r=[Rle], w=[Rc["lam"]])
        OP("dve", lambda e: e.tensor_scalar(out=lam[:, 0:1], in0=lam[:, 0:1], scalar1=LAM_INIT, scalar2=None,
                                            op0=ALU.add), r=[Rc["lam"]], w=[Rc["lam"]])
        OP("dve", lambda e: e.tensor_scalar(out=lam[:, 1:2], in0=lam[:, 0:1], scalar1=-1.0, scalar2=None,
                                            op0=ALU.mult), r=[Rc["lam"]], w=[Rc["lam"]])
        OP("dve", lambda e: e.tensor_scalar(out=gsub, in0=gsub, scalar1=1.0 - LAM_INIT, scalar2=None, op0=ALU.mult),
           r=[Rgs], w=[Rc["gsub"]])
        if stop_after == "p0c":
            S.barrier()
            S.emit()
            return nc
        cpb = bank(0)[:, 0:4 * 34].rearrange("p (c j) -> p c j", c=4)
        for ct in range(4):
            TR(cpb[:, ct, :], ptm[0:34, ct * 128:(ct + 1) * 128], ident_f[0:34, 0:34],
               r=[Rptm, Rc["ident_f"]], w=[RB[0]])
        OP("dve", lambda e: e.tensor_copy(out=chanP, in_=cpb), r=[], w=[RB[0], Rc["chanP"]])
        OP("dve", lambda e: e.tensor_scalar(out=chanP[:, :, 32:34], in0=chanP[:, :, 32:34], scalar1=0.5, scalar2=None,
                                            op0=ALU.mult), r=[Rc["chanP"]], w=[Rc["chanP"]])

        if stop_after == "p0d":
            S.barrier()
            S.emit()
            return nc
        wslot = S.slot(group=True)
        Rsc = {n: R("sc_" + n, True) for n in ["out", "ff1", "ff2", "gate", "ple"]}

        def v2k(ap):
            R_, C_ = ap.shape
            if C_ >= 2048:
                return ap.rearrange("r (a b) -> (r a) b", b=2048)
            return ap.rearrange("(r a) c -> r (a c)", a=2048 // C_)

        def conv_scratch():
            DMA("pool", v2k(sc_out), v2k(w_out), wslot, w=[Rsc["out"]])
            a1, b1 = v2k(sc_ff1), v2k(w_ff1)
            for i in range(4):
                DMA("pool", a1[i * 512:(i + 1) * 512, :], b1[i * 512:(i + 1) * 512, :], wslot, w=[Rsc["ff1"]])
            a2, b2_ = v2k(sc_ff2), v2k(w_ff2)
            for i in range(4):
                DMA("pool", a2[i * 512:(i + 1) * 512, :], b2_[i * 512:(i + 1) * 512, :], wslot, w=[Rsc["ff2"]])
            DMA("pool", v2k(sc_gate), v2k(w_gate), wslot, w=[Rsc["gate"]])
            DMA("pool", v2k(sc_ple), v2k(w_ple), wslot, w=[Rsc["ple"]])

        WGROUPS = [
            [(0, 512)], [(512, 512)], [(1024, 512)],
            [(1536, 128), (2048, 128), (1664, 128), (2176, 128)],
            [(1792, 128), (2304, 128), (1920, 128), (2432, 128)],
        ]
        Rw = [R("wst0"), R("wst1")]
        wsl = [S.slot(), S.slot()]
        w_in_v = w_in.rearrange("(c p) n -> p c n", p=128)

        def load_wgroup(g):
            b = g % 2
            o = 0
            for (c0, n) in WGROUPS[g]:
                DMA("pool", wst[b][:, :, o:o + n], w_in_v[:, :, c0:c0 + n], wsl[b], w=[Rw[b]])
                o += n

        load_wgroup(0)
        load_wgroup(1)

        if stop_after == "p0":
            S.barrier()
            S.emit()
            return nc
        xnT = XY
        Rxst = [R("xst0"), R("xst1")]
        Rxs = R("xs")
        xsl = [S.slot(), S.slot(), S.slot()]
        Rxn = [R("xn0"), R("xn1")]
        RxnT = [R(f"xnT{i}") for i in range(NT)]
        Rjunk = R("junk")
        stat = small[:, 404:404 + 32]
        Rstat = [R(f"stat{i}") for i in range(8)]

        def rstd_chain(src_ap, src_res, n, slot_i, scale):
            c = stat[:, slot_i * 4:slot_i * 4 + 4]
            rs = Rstat[slot_i]
            OP("act", lambda e: e.activation(out=junk[:, 0:n], in_=src_ap, func=AF.Square, accum_out=c[:, 0:1]),
               r=[src_res], w=[Rjunk, rs])
            OP("dve", lambda e: e.tensor_scalar(out=c[:, 1:2], in0=c[:, 0:1], scalar1=scale, scalar2=EPS,
                                                op0=ALU.mult, op1=ALU.add), r=[rs], w=[rs])
            OP("pool", lambda e: e.tensor_tensor(out=c[:, 2:3], in0=c[:, 1:2], in1=nhalf[:, 0:1], op=ALU.pow),
               r=[rs, Rc["nhalf"]], w=[rs])
            return c[:, 2:3], rs

        OP("pool", lambda e: e.memset(xs, 0.0), w=[Rxs])
        for bl in range(4):
            DMA("sp", xs[bl * 32:bl * 32 + 4, :], x_s[bl * 4:bl * 4 + 4, :], xsl[2], w=[Rxs])

        tile_order = [16] + list(range(16))
        for n_i, i in enumerate(tile_order):
            if i == 16:
                xt, rx = xs, Rxs
            else:
                b = i % 2
                xt, rx = xst[b], Rxst[b]
                if i == 1:
                    DMA("sp", xt, x_p[i * 128:(i + 1) * 128, :], xsl[b], r=[], w=[rx, Rptm])
                else:
                    DMA("sp", xt, x_p[i * 128:(i + 1) * 128, :], xsl[b], w=[rx])
            rstd, rs = rstd_chain(xt, rx, 1024, n_i % 4, 1.0 / D)
            b2 = n_i % 2
            OP("dve", lambda e, xt=xt, rstd=rstd, b2=b2: e.scalar_tensor_tensor(
                out=xn[b2], in0=xt, scalar=rstd, in1=gpre, op0=ALU.mult, op1=ALU.mult),
               r=[rx, rs, Rc["gpre"]], w=[Rxn[b2]])
            pb = n_i % 2
            tb = bank_bf(pb)[:, 0:1024].rearrange("p (c n) -> p c n", c=8)
            for c in range(8):
                TR(tb[:, c, :], xn[b2][:, c * 128:(c + 1) * 128], ident_bf, r=[Rxn[b2], Rc["ident_bf"]], w=[RB[pb]])
            OP("act", lambda e, i=i, tb=tb: e.copy(out=xnT[:, :, i * 128:(i + 1) * 128], in_=tb),
               r=[], w=[RB[pb], RxnT[i]])

        if stop_after == "p1":
            S.barrier()
            S.emit()
            return nc
        COLB = [(0, 512), (512, 512), (1024, 512), (1536, 512), (2048, 128)]

        def tiles_of(cb):
            c0, n = cb
            return list(range(c0 // 128, (c0 + n) // 128))

        RqT = [R(f"qT{i}") for i in range(NT)]
        RkT = [R(f"kT{i}") for i in range(NT)]
        Rv = [R(f"v{i}") for i in range(NT)]
        RuT = [R(f"uT{i}") for i in range(NT)]
        Rupad = R("upad")
        Rkvf = [R(f"kvf{i}") for i in range(3)]
        kvsl = [S.slot() for _ in range(3)]
        Rkbf = [R("kbf0"), R("kbf1")]
        Rgth = [R("gth0"), R("gth1")]
        Rgah = [R("gah0"), R("gah1")]
        nb = [0]

        nbmod = [8]

        def nextbank():
            nb[0] = (nb[0] + 1) % nbmod[0]
            return nb[0]

        OP("pool", lambda e: e.memset(v_ext[:, :, :, 128:129], 1.0), w=[Rc["ones"]])
        OP("pool", lambda e: e.memset(uT[:, :, 0:30], 0.0), w=[Rupad])

        wq = wst[0]
        for h in range(4):
            for cb in COLB:
                c0, n = cb
                b = nextbank()
                for c in range(8):
                    MM(bank(b)[:, 0:n], wq[:, c, h * 128:(h + 1) * 128], xnT[:, c, c0:c0 + n], c == 0, c == 7,
                       r=[Rw[0]] + [RxnT[i] for i in tiles_of(cb)], w=[RB[b]])
                eng = "act" if (h + c0 // 512) % 2 == 0 else "dve"
                if eng == "act":
                    OP("act", lambda e, b=b, h=h, c0=c0, n=n: e.copy(out=qT[:, h, c0:c0 + n], in_=bank(b)[:, 0:n]),
                       w=[RB[b]] + [RqT[i] for i in tiles_of(cb)])
                else:
                    OP("dve", lambda e, b=b, h=h, c0=c0, n=n: e.tensor_copy(out=qT[:, h, c0:c0 + n],
                                                                           in_=bank(b)[:, 0:n]),
                       w=[RB[b]] + [RqT[i] for i in tiles_of(cb)])
        load_wgroup(2)

        def out_rows(dst_p, dst_s, i, src, slot, rres):
            if i < 16:
                DMA("sp", dst_p[i * 128:(i + 1) * 128, :], src, slot, r=[rres], is_output=True)
            else:
                for bl in range(4):
                    DMA("sp", dst_s[bl * 4:bl * 4 + 4, :], src[bl * 32:bl * 32 + 4, :], slot, r=[rres],
                        is_output=True)

        wk = wst[1]
        for n_i, i in enumerate(tile_order):
            b = nextbank()
            for c in range(8):
                MM(bank(b), xnT[:, c, i * 128:(i + 1) * 128], wk[:, c, :], c == 0, c == 7,
                   r=[Rw[1], RxnT[i]], w=[RB[b]])
            s4 = n_i % 3
            OP("act", lambda e, b=b, s4=s4: e.copy(out=kvf[s4], in_=bank(b)), w=[RB[b], Rkvf[s4]])
            s2 = n_i % 2
            OP("dve", lambda e, b=b, s2=s2: e.tensor_copy(out=kbf[s2], in_=bank(b)), w=[RB[b], Rkbf[s2]])
            out_rows(k_p, k_s, i, kvf[s4], kvsl[s4], Rkvf[s4])
            b2 = nextbank()
            tb = bank_bf(b2)[:, 0:512].rearrange("p (h n) -> p h n", h=4)
            for h in range(4):
                TR(tb[:, h, :], kbf[s2][:, h * 128:(h + 1) * 128], ident_bf, r=[Rkbf[s2], Rc["ident_bf"]],
                   w=[RB[b2]])
            OP("dve", lambda e, i=i, tb=tb: e.tensor_copy(out=kT[:, :, i * 128:(i + 1) * 128], in_=tb),
               w=[RB[b2], RkT[i]])
        load_wgroup(3)

        wv = wst[0]
        for n_i, i in enumerate(tile_order):
            b = nextbank()
            for c in range(8):
                MM(bank(b), xnT[:, c, i * 128:(i + 1) * 128], wv[:, c, :], c == 0, c == 7,
                   r=[Rw[0], RxnT[i]], w=[RB[b]])
            s4 = (n_i + 1) % 3
            OP("act", lambda e, b=b, s4=s4: e.copy(out=kvf[s4], in_=bank(b)), w=[RB[b], Rkvf[s4]])
            OP("dve", lambda e, b=b, i=i: e.tensor_copy(out=v_ext[:, i, :, 0:128],
                                                        in_=bank(b).rearrange("p (h e) -> p h e", h=4)),
               w=[RB[b], Rv[i]])
            out_rows(v_p, v_s, i, kvf[s4], kvsl[s4], Rkvf[s4])
        load_wgroup(4)

        for g in (3, 4):
            wg_ = wst[g % 2]
            for j in range(2):
                ct = (g - 3) * 2 + j
                for cb in COLB:
                    c0, n = cb
                    ba = nextbank()
                    for c in range(8):
                        MM(bank(ba)[:, 0:n], wg_[:, c, j * 256:j * 256 + 128], xnT[:, c, c0:c0 + n], c == 0, c == 7,
                           r=[Rw[g % 2]] + [RxnT[i] for i in tiles_of(cb)], w=[RB[ba]])
                    bg = nextbank()
                    for c in range(8):
                        MM(bank(bg)[:, 0:n], wg_[:, c, j * 256 + 128:j * 256 + 256], xnT[:, c, c0:c0 + n], c == 0,
                           c == 7, r=[Rw[g % 2]] + [RxnT[i] for i in tiles_of(cb)], w=[RB[bg]])
                    s2 = (c0 // 512) % 2
                    OP("act", lambda e, bg=bg, n=n, s2=s2: e.activation(out=gth[s2][:, 0:n], in_=bank(bg)[:, 0:n],
                                                                        func=AF.Tanh, scale=0.5),
                       w=[RB[bg], Rgth[s2]])
                    OP("act", lambda e, ba=ba, n=n, s2=s2: e.activation(out=gah[s2][:, 0:n], in_=bank(ba)[:, 0:n],
                                                                        func=AF.Copy, scale=0.5),
                       w=[RB[ba], Rgah[s2]])
                    OP("dve", lambda e, ct=ct, c0=c0, n=n, s2=s2: e.scalar_tensor_tensor(
                        out=uT[:, ct, 30 + c0:30 + c0 + n], in0=gth[s2][:, 0:n], scalar=1.0, in1=gah[s2][:, 0:n],
                        op0=ALU.add, op1=ALU.mult),
                       r=[Rgth[s2], Rgah[s2]], w=[RuT[i] for i in tiles_of(cb)])

        conv_scratch()

        if stop_after in ("inproj", "p2s1", "p2s2", "p2s3"):
            S.barrier()
            S.emit()
            return nc


        def finish():
            if dbg_ym is not None and stop_after not in ("d1", "d3", "d4"):
                S.barrier()
                dsl = S.slot()
                S.dma("sp", lambda e: e.dma_start(out=dbg_ym, in_=XY.rearrange("p c n -> p (c n)")), dsl,
                      is_output=True)
            S.barrier()
            S.emit()
            return nc

        nbmod[0] = 4
        S.barrier()
        ymixT = XY
        Rym = [R(f"ym{i}") for i in range(NT)]
        Rymc = [R(f"ymc{i}") for i in range(NT)]
        epi = small[:, 440:440 + 24]
        Repi = R("epi")
        stsl = S.slot()
        gsl = S.slot()

        def mkap(base, dims):
            part = list(base.ap[0])
            return bass.AP(base.tensor, base.offset, [part] + [[s, c] for (s, c) in dims])

        Rcst = R("cstage")
        b = nextbank()
        cpv = bank(b)[0:30, :].rearrange("p (c n) -> p c n", c=4)
        for ct in range(4):
            TR(cpv[:, ct, :], uT[:, ct, 2048:2078], ident_f, r=[RuT[15], Rc["ident_f"]], w=[RB[b]])
        OP("act", lambda e, b=b: e.copy(out=cstage[0:30, 0:512], in_=bank(b)[0:30, :]), w=[RB[b], Rcst])
        DMA("sp", c_p, cstage[0:30, 0:512], gsl, r=[Rcst], is_output=True)
        Rsts = R("ststage")
        Rus = R("us_ext")
        DMA("sp", ststage[0:120, 0:512], st_conv, stsl, w=[Rsts])
        b = nextbank()
        for ct in range(4):
            TR(bank(b)[:, ct * 120:(ct + 1) * 120], ststage[0:120, ct * 128:(ct + 1) * 128], ident_f[0:120, 0:120],
               r=[Rsts, Rc["ident_f"]], w=[RB[b]])
        OP("dve", lambda e, b=b: e.tensor_copy(
            out=us_ext[:, :, :, 0:30], in_=bank(b)[:, 0:480].rearrange("p (c b j) -> p c b j", c=4, b=4)),
           w=[RB[b], Rus])
        u_s_view = uT[:, :, 30 + 2048:30 + 2176].rearrange("p c (b j) -> p c b j", b=4)[:, :, :, 0:4]
        OP("dve", lambda e: e.tensor_copy(out=us_ext[:, :, :, 30:34], in_=u_s_view), r=[RuT[16]], w=[Rus])
        for bl in range(4):
            DMA("sp", c_s[bl * 30:bl * 30 + 26, :], st_conv[bl * 30 + 4:bl * 30 + 30, :], gsl, is_output=True)
        Rustm = R("ustm")
        b = nextbank()
        for ct in range(4):
            TR(bank(b)[:, ct * 128:(ct + 1) * 128], uT[:, ct, 30 + 2048:30 + 2176], ident_f,
               r=[RuT[16], Rc["ident_f"]], w=[RB[b]])
        OP("act", lambda e, b=b: e.copy(out=ustm[:, 0:512], in_=bank(b)), w=[RB[b], Rustm])
        for bl in range(4):
            DMA("sp", c_s[bl * 30 + 26:bl * 30 + 30, :], ustm[bl * 32:bl * 32 + 4, 0:512], gsl, r=[Rustm],
                is_output=True)
        RaccS = R("accS")
        Rsp = R("sprod")
        for ct in range(4):
            base = us_ext[:, ct, :, :]
            win = mkap(base, [(34, 4), (1, 4), (1, 31)])
            wb = mkap(chanP[:, ct, 0:31], [(0, 4), (0, 4), (1, 31)])
            OP("dve", lambda e, win=win, wb=wb: e.tensor_tensor(out=sprod, in0=win, in1=wb, op=ALU.mult),
               r=[Rus, Rc["chanP"]], w=[Rsp])
            OP("dve", lambda e, ct=ct: e.tensor_reduce(out=accS[:, ct, :, :], in_=sprod, axis=AX.X, op=ALU.add),
               r=[Rsp], w=[RaccS])
            OP("dve", lambda e, ct=ct: e.tensor_scalar(out=accS[:, ct, :, :], in0=accS[:, ct, :, :],
                                                       scalar1=chanP[:, ct, 31:32], scalar2=None, op0=ALU.add),
               r=[RaccS, Rc["chanP"]], w=[RaccS])

        if stop_after == "convout":
            return finish()

        Rpt = R("pt")
        ptsl = S.slot()
        DMA("sp", pt_i, ptab.partition_broadcast(128), ptsl, w=[Rpt])
        Ridx = R("idx", True)
        OP("dve", lambda e: e.tensor_copy(out=pt_f, in_=pt_i), r=[Rpt], w=[Rpt])
        OP("dve", lambda e: e.tensor_scalar(out=idx_f, in0=pt_f, scalar1=128.0, scalar2=iop_f[:, 0:1],
                                            op0=ALU.mult, op1=ALU.add), r=[Rpt, Rc["iop"]], w=[Rpt])
        OP("dve", lambda e: e.tensor_copy(out=idx_i, in_=idx_f), r=[Rpt], w=[Ridx])
        Rqb = R("qblk", True)
        OP("pool", lambda e: e.memset(qblk, 0.0), w=[Rqb])
        for c in range(2):
            qsrc = mkap(qT[c * 64:(c + 1) * 64, 0, 2048:2052], [(32, 4), (NC_, 4), (1, 4)])
            OP("dve", lambda e, c=c, qsrc=qsrc: e.tensor_copy(out=qblk[c * 64:(c + 1) * 64, :, :, c * 4:(c + 1) * 4],
                                                             in_=qsrc), r=[RqT[16]], w=[Rqb])
        Rstmp = R("s_tmp")
        tA = s_tmp[:, 0:256].bitcast(BF16).rearrange("p (r t) -> p r t", r=2)
        RtA = Rstmp
        OP("pool", lambda e: e.iota(Atab[0:1, 0, :], pattern=[[1, 128]], base=0, channel_multiplier=0,
                                    allow_small_or_imprecise_dtypes=True), w=[Rc["Atab"]])
        OP("pool", lambda e: e.iota(Atab[0:1, 1, :], pattern=[[0, 4], [1, 32]], base=0, channel_multiplier=0,
                                    allow_small_or_imprecise_dtypes=True), w=[Rc["Atab"]])
        OP("pool", lambda e: e.memset(tA[0:1, 0, :], 1.0), w=[RtA])
        tsl = S.slot()
        DMA("sp", Atab[1:2, 0:2, :], tA[0:1, 0, :].rearrange("p (r t) -> p r t", r=2), tsl, r=[RtA], w=[Rc["Atab"]])
        tB = s_tmp[:, 256:256 + 65 * 16].bitcast(BF16).rearrange("p (j c) -> p j c", j=65)
        tJ = s_tmp[:, 1296:1296 + 65]
        RtB = Rstmp
        RtJ = Rstmp
        OP("pool", lambda e: e.iota(tJ[0:1, :], pattern=[[1, 65]], base=-64, channel_multiplier=0,
                                    allow_small_or_imprecise_dtypes=True), w=[RtJ])
        for h in range(4):
            OP("pool", lambda e, h=h: e.memset(Btab[0:1, :, h * 8:(h + 1) * 8], 8.0 * SLOPES[h]), w=[Rc["Btab"]])
            OP("pool", lambda e, h=h: e.tensor_scalar(
                out=tB[0:1, :, h * 8:(h + 1) * 8], in0=tJ[0:1, :].unsqueeze(2).to_broadcast([1, 65, 8]),
                scalar1=1024.0 * SLOPES[h], scalar2=None, op0=ALU.mult), r=[RtJ], w=[RtB])
        DMA("sp", Btab[1:2, :, :], tB[0:1, :, :], tsl, r=[RtB], w=[Rc["Btab"]])
        sm4 = smask.rearrange("p (b g t) -> p b g t", b=4, g=8)
        OP("pool", lambda e: e.memset(smask, 1.0), w=[Rc["smask"]])
        OP("pool", lambda e: e.affine_select(out=sm4, in_=sm4, compare_op=ALU.is_ge, fill=0.0, base=0,
                                             pattern=[[-32, 4], [0, 8], [0, 4]], channel_multiplier=1),
           r=[Rc["smask"]], w=[Rc["smask"]])
        OP("pool", lambda e: e.affine_select(out=sm4, in_=sm4, compare_op=ALU.is_ge, fill=0.0, base=0,
                                             pattern=[[32, 4], [0, 8], [1, 4]], channel_multiplier=-1),
           r=[Rc["smask"]], w=[Rc["smask"]])
        Dpos = s_tmp[0:8, 1364:1368]
        Dneg = s_tmp[0:8, 1368:1372]
        RD = Rstmp
        OP("pool", lambda e: e.memset(s_tmp[0:8, 1364:1372], 1.0), w=[RD])
        OP("pool", lambda e: e.affine_select(out=Dpos, in_=Dpos, compare_op=ALU.is_equal, fill=0.0, base=0,
                                             pattern=[[-1, 4]], channel_multiplier=1), r=[RD], w=[RD])
        OP("pool", lambda e: e.affine_select(out=Dneg, in_=Dneg, compare_op=ALU.is_equal, fill=0.0, base=-4,
                                             pattern=[[-1, 4]], channel_multiplier=1), r=[RD], w=[RD])
        OP("dve", lambda e: e.scalar_tensor_tensor(out=Dmat[0:8, 0:4], in0=Dneg, scalar=lam[0:8, 1:2], in1=Dpos,
                                                   op0=ALU.mult, op1=ALU.add), r=[RD, Rc["lam"]], w=[Rc["Dmat"]])
        Rkpg = [R(f"kpg{s}") for s in range(NS)]
        Rvpg = [R(f"vpg{s}") for s in range(NS)]
        ksl = [S.slot() for _ in range(NS)]
        vsl = [S.slot() for _ in range(NS)]
        RkTp = [R("kTp0"), R("kTp1")]
        RPTs = [R("PTs0"), R("PTs1")]
        ROsb = R("Osb")
        cache_v3 = cache_v.rearrange("r (h e) -> r h e", h=4)
        MB = [6, 7]
        OB = [4, 5]

        def page_dma(n):
            bl, j = divmod(n, NPG)
            s = n % NS
            col = bl * NPG + j
            S.dma("pool", lambda e: e.indirect_dma_start(
                out=kpg[s], out_offset=None, in_=cache_k,
                in_offset=bass.IndirectOffsetOnAxis(ap=idx_i[:, col:col + 1], axis=0)),
                ksl[s], reads=[Ridx], writes=[Rkpg[s]])
            S.dma("pool", lambda e: e.indirect_dma_start(
                out=vpg[s], out_offset=None, in_=cache_v,
                in_offset=bass.IndirectOffsetOnAxis(ap=idx_i[:, col:col + 1], axis=0)),
                vsl[s], reads=[Ridx], writes=[Rvpg[s]])

        PAGES = [(bl, j) for bl in range(4) for j in range(NPG + 1)]
        NPT = 4 * NPG

        def stageA(k):
            bl, j = PAGES[k]
            if j == NPG:
                return
            a = k % 2
            mb = MB[a]
            n = bl * NPG + j
            s = n % NS
            ktp = bank_bf(mb)[:, 0:512].rearrange("p (h t) -> p h t", h=4)
            for h in range(4):
                TR(ktp[:, h, :], kpg[s][:, h * 128:(h + 1) * 128], ident_bf,
                   r=[Rkpg[s], Rc["ident_bf"]], w=[RB[mb]])
            if k % 2 == 0:
                OP("act", lambda e, a=a, ktp=ktp: e.copy(out=kTp[a], in_=ktp), w=[RB[mb], RkTp[a]])
            else:
                OP("dve", lambda e, a=a, ktp=ktp: e.tensor_copy(out=kTp[a], in_=ktp), w=[RB[mb], RkTp[a]])

        def stageB(k):
            bl, j = PAGES[k]
            a = k % 2
            mb = MB[a]
            sT = bank(mb)[:, 256:288]
            new = (j == NPG)
            for h in range(4):
                if not new:
                    MM(sT[:, h * 8:(h + 1) * 8], kTp[a][:, h, :], qblk[:, bl, h, :], h == 0, False,
                       r=[RkTp[a], Rqb], w=[RB[mb]], skip_group_check=True)
                else:
                    MM(sT[:, h * 8:(h + 1) * 8], kT[:, h, 2048:2176], qblk[:, bl, h, :], h == 0, False,
                       r=[RkT[16], Rqb], w=[RB[mb]], skip_group_check=True)
            MM(sT, Atab[0:2, 1 if new else 0, :], Btab[0:2, j, :], False, True,
               r=[Rc["Atab"], Rc["Btab"]], w=[RB[mb]], skip_group_check=True)
            OP("act", lambda e, a=a, sT=sT: e.activation(out=PTs[a], in_=sT, func=AF.Exp, scale=0.125),
               w=[RB[mb], RPTs[a]])
            if new:
                OP("dve", lambda e, a=a, bl=bl: e.tensor_tensor(
                    out=PTs[a], in0=PTs[a], in1=smask[:, bl * 32:(bl + 1) * 32], op=ALU.mult),
                   r=[RPTs[a], Rc["smask"]], w=[RPTs[a]])

        def stageC(k):
            bl, j = PAGES[k]
            a = k % 2
            new = (j == NPG)
            if not new:
                n = bl * NPG + j
                s = n % NS
                vsrc = [vpg[s][:, h * 128:(h + 1) * 128] for h in range(4)]
                vres = Rvpg[s]
            else:
                vsrc = [v_ext[:, 16, h, 0:128] for h in range(4)]
                vres = Rv[16]
            for h in range(4):
                ob = OB[h // 2]
                ov = bank(ob)[0:8, 0:258].rearrange("p (q e) -> p q e", q=2)
                MM(ov[:, h % 2, 0:128], PTs[a][:, h * 8:(h + 1) * 8], vsrc[h], (j == 0 and h % 2 == 0), new,
                   r=[RPTs[a], vres], w=[RB[ob]], skip_group_check=True)
                MM(ov[:, h % 2, 128:129], PTs[a][:, h * 8:(h + 1) * 8], tri[:, 127:128], False, new,
                   r=[RPTs[a], Rc["tri"]], w=[RB[ob]], skip_group_check=True)
            if not new and n + NS < NPT:
                page_dma(n + NS)
            if new:
                for q2 in range(2):
                    OP("dve", lambda e, q2=q2: e.tensor_copy(
                        out=Osb[0:8, 2 * q2:2 * q2 + 2, :],
                        in_=bank(OB[q2])[0:8, 0:258].rearrange("p (q e) -> p q e", q=2)),
                       w=[RB[OB[q2]], ROsb])
                rl = s_tmp[0:8, 1280:1284]
                On = s_tmp[0:8, 0:512].rearrange("p (h e) -> p h e", h=4)
                OP("dve", lambda e: e.reciprocal(out=rl.unsqueeze(2), in_=Osb[0:8, :, 128:129]),
                   r=[ROsb], w=[Rstmp])
                OP("dve", lambda e: e.tensor_tensor(out=On, in0=Osb[0:8, :, 0:128],
                                                    in1=rl.unsqueeze(2).to_broadcast([8, 4, 128]), op=ALU.mult),
                   r=[ROsb, Rstmp], w=[Rstmp])
                mb2 = MB[k % 2]
                MM(bank(mb2)[0:4, :], Dmat[0:8, 0:4], s_tmp[0:8, 0:512], True, True,
                   r=[Rstmp, Rc["Dmat"]], w=[RB[mb2]])
                o_s = s_tmp[0:4, 512:1024].rearrange("p (h e) -> p h e", h=4)
                osq_s = s_tmp[0:4, 0:512].rearrange("p (h e) -> p h e", h=4)
                st4 = s_tmp[0:4, 1288:1300]
                OP("dve", lambda e, mb2=mb2: e.tensor_copy(out=s_tmp[0:4, 512:1024], in_=bank(mb2)[0:4, :]),
                   w=[RB[mb2], Rstmp])
                OP("dve", lambda e: e.tensor_tensor(out=osq_s, in0=o_s, in1=o_s, op=ALU.mult),
                   r=[Rstmp], w=[Rstmp])
                OP("dve", lambda e: e.tensor_reduce(out=st4[:, 0:4], in_=osq_s, axis=AX.X, op=ALU.add),
                   r=[Rstmp], w=[Rstmp])
                OP("dve", lambda e: e.tensor_scalar(out=st4[:, 4:8], in0=st4[:, 0:4], scalar1=1.0 / 128,
                                                    scalar2=EPS, op0=ALU.mult, op1=ALU.add),
                   r=[Rstmp], w=[Rstmp])
                OP("pool", lambda e: e.tensor_tensor(out=st4[:, 8:12], in0=st4[:, 4:8], in1=nhalf[0:4, 0:4],
                                                     op=ALU.pow), r=[Rstmp, Rc["nhalf"]], w=[Rstmp])
                OP("dve", lambda e: e.tensor_tensor(
                    out=osq_s, in0=o_s, in1=st4[:, 8:12].unsqueeze(2).to_broadcast([4, 4, 128]), op=ALU.mult),
                   r=[Rstmp], w=[Rstmp])
                ya_s = s_tmp[0:4, 1024:1024 + 256].bitcast(BF16).rearrange("p (h e) -> p h e", h=4)
                OP("dve", lambda e: e.tensor_tensor(
                    out=ya_s, in0=osq_s, in1=gsub[0:4, :].unsqueeze(1).to_broadcast([4, 4, 128]), op=ALU.mult),
                   r=[Rstmp, Rc["gsub"]], w=[Rstmp])
                mb3 = MB[(k + 1) % 2]
                tq = bank_bf(mb3)[:, 0:16].rearrange("p (h t) -> p h t", h=4)
                for h in range(4):
                    TR(tq[:, h, :], ya_s[:, h, :], ident_bf[0:4, 0:4], r=[Rstmp, Rc["ident_bf"]],
                       w=[RB[mb3]])
                OP("dve", lambda e, bl=bl, tq=tq: e.tensor_copy(
                    out=ymixT[:, 0:4, 2048 + bl * 32:2048 + bl * 32 + 4], in_=tq),
                   w=[RB[mb3], Rym[16]])

        def sample_gen():
            for n in range(min(NS, NPT)):
                page_dma(n)
            NP = len(PAGES)
            for t in range(NP + 2):
                if 0 <= t - 2 < NP:
                    stageC(t - 2)
                if 0 <= t - 1 < NP:
                    stageB(t - 1)
                if t < NP:
                    stageA(t)
                yield

        sgen = sample_gen()
        sdone = [False]

        def tick(k=1):
            for _ in range(k):
                if sdone[0]:
                    return
                try:
                    next(sgen)
                except StopIteration:
                    sdone[0] = True

        OP("pool", lambda e: e.memset(ymixT[:, :, 2048:2176], 0.0), w=[Rym[16], Rymc[16]])

        if stop_after == "sample":
            tick(10 ** 6)
            return finish()

        Rcacc = R("cacc")
        Rz = [R("z0"), R("z1")]
        Rsth = [R("sth0"), R("sth1")]
        Rszz = [R("szz0"), R("szz1")]
        lnst = small[:, 470:470 + 12]
        Rln = R("lnst")
        lni = [0]

        def ln_tile(i, cols):
            j = lni[0] % 2
            lni[0] += 1
            ba = 0
            for ct in range(4):
                TR(bank(ba)[:, ct * 128:(ct + 1) * 128], cacc[:, ct, cols:cols + 128], ident_f,
                   r=[Rcacc, Rc["ident_f"]], w=[RB[ba]])
            OP("dve", lambda e, ba=ba: e.bn_stats(out=lnst[:, 0:6], in_=bank(ba)), w=[RB[ba], Rln])
            OP("dve", lambda e: e.bn_aggr(out=lnst[:, 6:8], in_=lnst[:, 0:6]), r=[Rln], w=[Rln])
            OP("dve", lambda e: e.tensor_scalar(out=lnst[:, 8:9], in0=lnst[:, 7:8], scalar1=EPS, scalar2=None,
                                                op0=ALU.add), r=[Rln], w=[Rln])
            OP("pool", lambda e: e.tensor_tensor(out=lnst[:, 9:10], in0=lnst[:, 8:9], in1=nhalf[:, 0:1], op=ALU.pow),
               r=[Rln, Rc["nhalf"]], w=[Rln])
            OP("dve", lambda e: e.tensor_scalar(out=lnst[:, 10:11], in0=lnst[:, 6:7], scalar1=lnst[:, 9:10],
                                                scalar2=-1.0, op0=ALU.mult, op1=ALU.mult), r=[Rln], w=[Rln])
            OP("act", lambda e, ba=ba, j=j: e.activation(out=zbuf[j][:, 0:512], in_=bank(ba), func=AF.Identity,
                                                         bias=lnst[:, 10:11], scale=lnst[:, 9:10]),
               r=[Rln], w=[RB[ba], Rz[j]])
            yield
            yield
            yield
            bb = 1
            for ct in range(4):
                TR(bank(bb)[:, ct * 128:(ct + 1) * 128], zbuf[j][:, ct * 128:(ct + 1) * 128], ident_f,
                   r=[Rz[j], Rc["ident_f"]], w=[RB[bb]])
            for ct in range(4):
                OP("act", lambda e, bb=bb, ct=ct, j=j: e.activation(
                    out=sth[j][:, ct, :], in_=bank(bb)[:, ct * 128:(ct + 1) * 128], func=AF.Tanh,
                    bias=chanP[:, ct, 33:34], scale=chanP[:, ct, 32:33]),
                   r=[Rc["chanP"]], w=[RB[bb], Rsth[j]])
                OP("act", lambda e, bb=bb, ct=ct, j=j: e.activation(
                    out=szz[j][:, ct, :], in_=bank(bb)[:, ct * 128:(ct + 1) * 128], func=AF.Identity,
                    bias=chanP[:, ct, 33:34], scale=chanP[:, ct, 32:33]),
                   r=[Rc["chanP"]], w=[RB[bb], Rszz[j]])
            OP("dve", lambda e, i=i, j=j: e.scalar_tensor_tensor(
                out=ymixT[:, 4:8, i * 128:(i + 1) * 128], in0=sth[j], scalar=1.0, in1=szz[j],
                op0=ALU.add, op1=ALU.mult), r=[Rsth[j], Rszz[j]], w=[Rymc[i]])

        def conv_gen():
            for tb4 in range(4):
                c0 = tb4 * 512
                for ct in range(4):
                    OP("dve", lambda e, ct=ct, c0=c0: e.tensor_scalar(
                        out=cacc[:, ct, :], in0=uT[:, ct, c0:c0 + 512], scalar1=chanP[:, ct, 0:1],
                        scalar2=chanP[:, ct, 31:32], op0=ALU.mult, op1=ALU.add),
                       r=[Rupad, Rc["chanP"]] + [RuT[i] for i in range(max(0, tb4 * 4 - 1), tb4 * 4 + 4)], w=[Rcacc])
                    yield
                    for jt in range(1, 31):
                        OP("dve", lambda e, ct=ct, c0=c0, jt=jt: e.scalar_tensor_tensor(
                            out=cacc[:, ct, :], in0=uT[:, ct, c0 + jt:c0 + jt + 512], scalar=chanP[:, ct, jt:jt + 1],
                            in1=cacc[:, ct, :], op0=ALU.mult, op1=ALU.add), r=[Rcacc], w=[Rcacc])
                        yield
                for k in range(4):
                    yield from ln_tile(tb4 * 4 + k, k * 128)
                    yield
            OP("dve", lambda e: e.memset(cacc[:, :, 0:128], 0.0), w=[Rcacc])
            OP("dve", lambda e: e.tensor_copy(
                out=cacc[:, :, 0:128].rearrange("p c (b j) -> p c b j", b=4)[:, :, :, 0:4], in_=accS),
               r=[RaccS], w=[Rcacc])
            yield from ln_tile(16, 0)
            yield

        cgen = conv_gen()
        cdone = [False]

        def ctick(k=1):
            for _ in range(k):
                if cdone[0]:
                    return
                try:
                    next(cgen)
                except StopIteration:
                    cdone[0] = True

        Rya = [R(f"ya{i}") for i in range(NT)]
        RPT = [R("PT0"), R("PT1"), R("PT2")]
        Ro0 = [R("o0a"), R("o0b")]
        Rt1 = R("t1")
        Ros = R("osum")
        sbi = [0]
        pti = [0]
        hq = [0]
        TICK_EVERY = int(os.environ.get("K_TICK", "2"))
        stepc = [0]
        carry = [None]
        for h in range(4):
            for Q in range(8):
                j2 = hq[0] % 2
                hq[0] += 1
                for c in range(2):
                    ob = 2 + c
                    Ov = bank(ob)[:, 0:258].rearrange("p (q e) -> p q e", q=2)

                    def pv_step(kt, pi, off, Ov=Ov, Q=Q, h=h, ob=ob):
                        for qs in range(2):
                            if kt <= 2 * Q + qs:
                                lc = qs * 128 - off
                                MM(Ov[:, qs, :], PT[pi][:, lc:lc + 128], v_ext[:, kt, h, :],
                                   (kt == 0 and qs == 0), kt == 2 * Q + qs,
                                   r=[RPT[pi], Rv[kt], Rc["ones"]], w=[RB[ob]], skip_group_check=True)

                    def epilogue(c=c, Ov=Ov, j2=j2, Q=Q, h=h, ob=ob):
                        if c == 0:
                            OP("dve", lambda e: e.reciprocal(out=epi[:, 0:2].unsqueeze(2), in_=Ov[:, :, 128:129]),
                               w=[RB[ob], Repi])
                            OP("dve", lambda e: e.tensor_tensor(
                                out=o0[j2], in0=Ov[:, :, 0:128],
                                in1=epi[:, 0:2].unsqueeze(2).to_broadcast([128, 2, 128]),
                                op=ALU.mult), r=[Repi], w=[RB[ob], Ro0[j2]])
                        else:
                            OP("dve", lambda e: e.reciprocal(out=epi[:, 2:4].unsqueeze(2), in_=Ov[:, :, 128:129]),
                               w=[RB[ob], Repi])
                            OP("dve", lambda e: e.tensor_scalar(out=epi[:, 4:6], in0=epi[:, 2:4], scalar1=lam[:, 1:2],
                                                                scalar2=None, op0=ALU.mult),
                               r=[Repi, Rc["lam"]], w=[Repi])
                            OP("dve", lambda e: e.tensor_tensor(
                                out=t1, in0=Ov[:, :, 0:128],
                                in1=epi[:, 4:6].unsqueeze(2).to_broadcast([128, 2, 128]),
                                op=ALU.mult), r=[Repi], w=[RB[ob], Rt1])
                            OP("dve", lambda e: e.tensor_tensor(out=osum, in0=o0[j2], in1=t1, op=ALU.add),
                               r=[Ro0[j2], Rt1], w=[Ros])
                            for qs in range(2):
                                OP("act", lambda e, qs=qs: e.activation(out=junk2[:, 0:128], in_=osum[:, qs, :],
                                                                        func=AF.Square,
                                                                        accum_out=epi[:, 6 + qs:7 + qs]),
                                   r=[Ros], w=[Rjunk, Repi])
                            OP("dve", lambda e: e.tensor_scalar(out=epi[:, 8:10], in0=epi[:, 6:8], scalar1=1.0 / 128,
                                                                scalar2=EPS, op0=ALU.mult, op1=ALU.add),
                               r=[Repi], w=[Repi])
                            OP("pool", lambda e: e.tensor_tensor(out=epi[:, 10:12], in0=epi[:, 8:10],
                                                                 in1=nhalf[:, 0:2], op=ALU.pow),
                               r=[Repi, Rc["nhalf"]], w=[Repi])
                            OP("dve", lambda e: e.tensor_tensor(
                                out=t1, in0=osum, in1=epi[:, 10:12].unsqueeze(2).to_broadcast([128, 2, 128]),
                                op=ALU.mult), r=[Ros, Repi], w=[Rt1])
                            OP("dve", lambda e: e.tensor_tensor(
                                out=y_attn[:, 2 * Q:2 * Q + 2, h * 128:(h + 1) * 128], in0=t1,
                                in1=gsub.unsqueeze(1).to_broadcast([128, 2, 128]), op=ALU.mult),
                               r=[Rt1, Rc["gsub"]], w=[Rya[2 * Q], Rya[2 * Q + 1]])

                    pend = None
                    for kt in range(2 * Q + 2):
                        off = max(0, kt * 128 - Q * 256)
                        ncol = 256 - off
                        sbk = sbi[0] % 2
                        sbi[0] += 1
                        MM(bank(sbk)[:, 0:ncol], kT[c * 64:(c + 1) * 64, h, kt * 128:(kt + 1) * 128],
                           qT[c * 64:(c + 1) * 64, h, Q * 256 + off:(Q + 1) * 256], True, True,
                           r=[RkT[kt], RqT[2 * Q], RqT[2 * Q + 1]], w=[RB[sbk]])
                        pi = pti[0] % 3
                        pti[0] += 1
                        dl = kt - 2 * Q + 14
                        OP("act", lambda e, pi=pi, ncol=ncol, sbk=sbk, h=h, dl=dl: e.activation(
                            out=PT[pi][:, 0:ncol], in_=bank(sbk)[:, 0:ncol], func=AF.Exp,
                            bias=biasT[:, h, dl:dl + 1], scale=0.125),
                           r=[Rc["biasT"]], w=[RB[sbk], RPT[pi]])
                        if kt >= 2 * Q:
                            OP("dve", lambda e, pi=pi: e.tensor_tensor(out=PT[pi][:, 0:128], in0=PT[pi][:, 0:128],
                                                                      in1=tri, op=ALU.mult),
                               r=[RPT[pi], Rc["tri"]], w=[RPT[pi]])
                        if kt == 0 and carry[0] is not None:
                            cpv, cargs, cepi = carry[0]
                            cpv(*cargs)
                            cepi()
                            carry[0] = None
                        if pend is not None:
                            pv_step(*pend)
                        pend = (kt, pi, off)
                        stepc[0] += 1
                        if TICK_EVERY > 0 and stepc[0] % TICK_EVERY == 0:
                            tick()
                        ctick()
                    carry[0] = (pv_step, pend, epilogue)
        if carry[0] is not None:
            cpv, cargs, cepi = carry[0]
            cpv(*cargs)
            cepi()
            carry[0] = None

        for i in range(16):
            b = nextbank()
            tb = bank_bf(b)[:, 0:512].rearrange("p (h n) -> p h n", h=4)
            for h in range(4):
                TR(tb[:, h, :], y_attn[:, i, h * 128:(h + 1) * 128], ident_bf, r=[Rya[i], Rc["ident_bf"]],
                   w=[RB[b]])
            if i % 2 == 0:
                OP("act", lambda e, i=i, tb=tb: e.copy(out=ymixT[:, 0:4, i * 128:(i + 1) * 128], in_=tb),
                   w=[RB[b], Rym[i]])
            else:
                OP("dve", lambda e, i=i, tb=tb: e.tensor_copy(out=ymixT[:, 0:4, i * 128:(i + 1) * 128], in_=tb),
                   w=[RB[b], Rym[i]])
            tick()

        ctick(10 ** 6)
        if stop_after == "attn":
            tick(10 ** 6)
            return finish()

        if stop_after == "mixer":
            tick(10 ** 6)
            return finish()

        tick(10 ** 6)
        S.barrier()
        gD = S.slot(group=True)
        Rg = R("gD", True)
        DMA("sp", gbc[0], g_post_mix.partition_broadcast(128), gD, w=[Rg])
        DMA("sp", gbc[1], g_pre_ffn.partition_broadcast(128), gD, w=[Rg])
        DMA("sp", gbc[2], g_post_ffn.partition_broadcast(128), gD, w=[Rg])

        RwD = [R(f"wD{i}") for i in range(4)]
        wDsl = [S.slot() for _ in range(4)]
        wq_ = []
        wi = [0]
        sc_out_v = sc_out.rearrange("(c p) n -> p c n", p=128)
        sc_ff1_v = sc_ff1.rearrange("(c p) n -> p c n", p=128)
        sc_ff2_v = sc_ff2.rearrange("(c p) n -> p c n", p=128)
        sc_gate_v = sc_gate.rearrange("(c p) n -> p c n", p=128)
        sc_ple_v = sc_ple.rearrange("(c p) n -> p c n", p=128)

        def chunk_list():
            L = []
            for half in range(2):
                L.append(("out", sc_out_v[:, :, half * 512:(half + 1) * 512], Rsc["out"], (8, 512)))
            for m in range(8):
                L.append(("ff1", sc_ff1_v[:, :, m * 512:(m + 1) * 512], Rsc["ff1"], (8, 512)))
            for half in range(2):
                for m in range(4):
                    L.append(("ff2", sc_ff2_v[:, m * 8:(m + 1) * 8, half * 512:(half + 1) * 512], Rsc["ff2"], (8, 512)))
            for half in range(2):
                L.append(("gate", sc_gate_v[:, :, half * 512:(half + 1) * 512], Rsc["gate"], (8, 512)))
            L.append(("ple", sc_ple_v, Rsc["ple"], (2, 1024)))
            return L

        pending = []
        issued = []

        def issue_one():
            if not pending:
                return
            kind, src, rsc, (a, n) = pending.pop(0)
            bi = wi[0] % 4
            wi[0] += 1
            view = wsD[bi][:, 0:a * n].rearrange("p (c n) -> p c n", c=a)
            DMA("sp", view, src, wDsl[bi], r=[rsc], w=[RwD[bi]])
            issued.append((bi, view))

        def get_chunk():
            bi, view = issued.pop(0)
            return view, RwD[bi]

        Rh = [R(f"h{i}") for i in range(4)]
        RhnT = [R(f"hnT{i}") for i in range(5)]
        Rf1 = R("f1T")
        Rfb = [R(f"fb{i}") for i in range(4)]
        Rxr = [R("xr0"), R("xr1")]
        xrsl = [S.slot(), S.slot()]
        Rpr = [R("pr0"), R("pr1")]
        prsl = [S.slot(), S.slot()]
        Rpb = [R("pb0"), R("pb1")]
        RpeT = [R("peT0"), R("peT1")]
        Rtmp = [R("tmpD0"), R("tmpD1"), R("tmpD2")]
        ysl = S.slot()
        Rhbf = [R("hbf0"), R("hbf1")]
        Rj3 = R("junk3")
        statD = small[:, 484:484 + 24]
        RstD = [R(f"stD{i}") for i in range(4)]

        def rstd_D(src_list, n_total, si):
            c = statD[:, si * 6:si * 6 + 6]
            rs = RstD[si]
            for k, (ap, res, bk) in enumerate(src_list):
                n = ap.shape[-1]
                if bk is None:
                    OP("act", lambda e, ap=ap, n=n, k=k: e.activation(out=junk3[:, 0:n], in_=ap, func=AF.Square,
                                                                      accum_out=c[:, k:k + 1]),
                       r=[res], w=[Rj3, rs])
                else:
                    OP("act", lambda e, ap=ap, n=n, k=k: e.activation(out=junk3[:, 0:n], in_=ap, func=AF.Square,
                                                                      accum_out=c[:, k:k + 1]),
                       r=[], w=[Rj3, rs] + [RB[x] for x in bk])
            if len(src_list) == 2:
                OP("dve", lambda e: e.tensor_tensor(out=c[:, 0:1], in0=c[:, 0:1], in1=c[:, 1:2], op=ALU.add),
                   r=[rs], w=[rs])
            OP("dve", lambda e: e.tensor_scalar(out=c[:, 2:3], in0=c[:, 0:1], scalar1=1.0 / n_total, scalar2=EPS,
                                                op0=ALU.mult, op1=ALU.add), r=[rs], w=[rs])
            OP("pool", lambda e: e.tensor_tensor(out=c[:, 3:4], in0=c[:, 2:3], in1=nhalf[:, 0:1], op=ALU.pow),
               r=[rs, Rc["nhalf"]], w=[rs])
            return c[:, 3:4], rs

        def load_rows(dst, dres, slot, src_p, src_s, i, width):
            if i < 16:
                DMA("sp", dst[:, 0:width], src_p[i * 128:(i + 1) * 128, :], slot, w=[dres])
            else:
                OP("pool", lambda e: e.memset(dst[:, 0:width], 0.0), w=[dres])
                for bl in range(4):
                    DMA("sp", dst[bl * 32:bl * 32 + 4, 0:width], src_s[bl * 4:bl * 4 + 4, :], slot, w=[dres])

        def to_T(src_bf, sres, dstT, dres, col, nchunk, pbk):
            tb = bank_bf(pbk)[:, 0:nchunk * 128].rearrange("p (c n) -> p c n", c=nchunk)
            for c in range(nchunk):
                TR(tb[:, c, :], src_bf[:, c * 128:(c + 1) * 128], ident_bf, r=[sres, Rc["ident_bf"]], w=[RB[pbk]])
            OP("act", lambda e: e.copy(out=dstT[:, 0:nchunk, col:col + 128], in_=tb), w=[RB[pbk], dres])

        BLOCKS = [[0, 1, 2, 3], [4, 5, 6, 7], [8, 9, 10, 11], [12, 13, 14, 15], [16]]
        if os.environ.get('K_BLKS'):
            BLOCKS = [BLOCKS[int(c)] for c in os.environ['K_BLKS'].split(',')]
        tcount = [0]
        for blk in BLOCKS:
            nt = len(blk)
            ncols = nt * 128
            pending.extend(chunk_list())
            while len(issued) < 3 and pending:
                issue_one()
            wo = [get_chunk(), get_chunk()]
            prevT = None
            for ti, i in enumerate(blk):
                j = tcount[0] % 2
                tcount[0] += 1
                pb2 = (0, 1) if j == 0 else (2, 3)
                mps = bank(pb2[0], 2)
                for half in range(2):
                    wv, wr = wo[half]
                    for c in range(8):
                        MM(bank(pb2[half]), ymixT[:, c, i * 128:(i + 1) * 128], wv[:, c, :], c == 0, c == 7,
                           r=[wr, Rym[i], Rymc[i]], w=[RB[pb2[half]]])
                if prevT is not None:
                    to_T(*prevT)
                load_rows(xrD[j], Rxr[j], xrsl[j], x_p, x_s, i, D)
                rstd, rs = rstd_D([(mps, None, pb2)], D, 0)
                OP("dve", lambda e, mps=mps, rstd=rstd: e.scalar_tensor_tensor(
                    out=tmpD[0], in0=mps, scalar=rstd, in1=gbc[0], op0=ALU.mult, op1=ALU.mult),
                   r=[rs, Rg], w=[RB[pb2[0]], RB[pb2[1]], Rtmp[0]])
                OP("dve", lambda e, ti=ti, j=j: e.tensor_tensor(out=hblk[:, ti, :], in0=tmpD[0], in1=xrD[j], op=ALU.add),
                   r=[Rtmp[0], Rxr[j]], w=[Rh[ti]])
                rstd2, rs2 = rstd_D([(hblk[:, ti, :], Rh[ti], None)], D, 1)
                OP("dve", lambda e, ti=ti, j=j, rstd2=rstd2: e.scalar_tensor_tensor(
                    out=hbfD[j], in0=hblk[:, ti, :], scalar=rstd2, in1=gbc[1], op0=ALU.mult, op1=ALU.mult),
                   r=[Rh[ti], rs2, Rg], w=[Rhbf[j]])
                prevT = (hbfD[j], Rhbf[j], hnT, RhnT[ti], ti * 128, 8, 4 + j)
            to_T(*prevT)
            if stop_after == "d1":
                S.barrier()
                dsl2 = S.slot()
                S.dma("sp", lambda e: e.dma_start(out=dbg_f, in_=hblk.rearrange("p i n -> p (i n)")), dsl2, is_output=True)
                return finish()
            issue_one()
            issue_one()
            rot = [0]
            for m in range(8):
                wv, wr = get_chunk()
                for f in range(4):
                    fc = 4 * m + f
                    bk = rot[0] % 4
                    rot[0] += 1
                    for c in range(8):
                        MM(bank(bk)[:, 0:ncols], wv[:, c, f * 128:(f + 1) * 128], hnT[:, c, 0:ncols], c == 0, c == 7,
                           r=[wr] + RhnT[0:nt], w=[RB[bk]])
                    sq = tmpD[1 + fc % 2]
                    rsq = Rtmp[1 + fc % 2]
                    OP("act", lambda e, bk=bk, sq=sq, ncols=ncols: e.activation(
                        out=sq[:, 0:ncols], in_=bank(bk)[:, 0:ncols], func=AF.Square), w=[RB[bk], rsq])
                    OP("dve", lambda e, bk=bk, sq=sq, fc=fc, ncols=ncols: e.scalar_tensor_tensor(
                        out=f1T[:, fc, 0:ncols], in0=bank(bk)[:, 0:ncols], scalar=0.0, in1=sq[:, 0:ncols],
                        op0=ALU.is_gt, op1=ALU.mult), r=[rsq], w=[RB[bk], Rf1])
                issue_one()
            for half in range(2):
                for m in range(4):
                    wv, wr = get_chunk()
                    for ti in range(nt):
                        for f in range(8):
                            fc = 8 * m + f
                            MM(bank(ti + 4 * half), f1T[:, fc, ti * 128:(ti + 1) * 128], wv[:, f, :],
                               (m == 0 and f == 0), (m == 3 and f == 7), r=[wr, Rf1], w=[RB[ti + 4 * half]])
                    issue_one()
                if half == 0:
                    for ti in range(nt):
                        OP("act", lambda e, ti=ti: e.copy(out=fblk[:, ti, :], in_=bank(ti)), w=[RB[ti], Rfb[ti]])
            for ti, i in enumerate(blk):
                rstd3, rs3 = rstd_D([(fblk[:, ti, :], Rfb[ti], None), (bank(4 + ti), None, (4 + ti,))], D, 2)
                OP("dve", lambda e, ti=ti, rstd3=rstd3: e.scalar_tensor_tensor(
                    out=tmpD[0][:, 0:512], in0=fblk[:, ti, :], scalar=rstd3, in1=gbc[2][:, 0:512],
                    op0=ALU.mult, op1=ALU.mult), r=[Rfb[ti], rs3, Rg], w=[Rtmp[0]])
                OP("dve", lambda e, ti=ti, rstd3=rstd3: e.scalar_tensor_tensor(
                    out=tmpD[0][:, 512:1024], in0=bank(4 + ti), scalar=rstd3, in1=gbc[2][:, 512:1024],
                    op0=ALU.mult, op1=ALU.mult), r=[rs3, Rg], w=[RB[4 + ti], Rtmp[0]])
                OP("dve", lambda e, ti=ti: e.tensor_tensor(out=hblk[:, ti, :], in0=hblk[:, ti, :], in1=tmpD[0],
                                                          op=ALU.add), r=[Rtmp[0], Rh[ti]], w=[Rh[ti]])
            if stop_after == "d3":
                S.barrier()
                dsl2 = S.slot()
                S.dma("sp", lambda e: e.dma_start(out=dbg_f, in_=hblk.rearrange("p i n -> p (i n)")), dsl2, is_output=True)
                return finish()
            wg2 = [get_chunk(), get_chunk()]
            wpv, wpr = get_chunk()
            jb = tcount[0]
            tcount[0] += nt

            def d4_front(ti, i):
                j = (jb + ti) % 2
                OP("act", lambda e, ti=ti, j=j: e.copy(out=hbfD[j], in_=hblk[:, ti, :]), r=[Rh[ti]], w=[Rhbf[j]])
                to_T(hbfD[j], Rhbf[j], hnT, RhnT[ti], ti * 128, 8, 4 + j)
                load_rows(prD[j], Rpr[j], prsl[j], p_p, p_s, i, 256)
                OP("dve", lambda e, j=j: e.tensor_copy(out=pbD[j], in_=prD[j]), r=[Rpr[j]], w=[Rpb[j]])
                to_T(pbD[j], Rpb[j], peT[j], RpeT[j], 0, 2, 6 + j)

            d4_front(0, blk[0])
            for ti, i in enumerate(blk):
                j = (jb + ti) % 2
                if ti + 1 < nt:
                    d4_front(ti + 1, blk[ti + 1])
                for half in range(2):
                    wv, wr = wg2[half]
                    for c in range(8):
                        MM(bank(half), hnT[:, c, ti * 128:(ti + 1) * 128], wv[:, c, :], c == 0, c == 7,
                           r=[wr, RhnT[ti]], w=[RB[half]])
                for half in range(2):
                    for c in range(2):
                        MM(bank(2 + half), peT[j][:, c, :], wpv[:, c, half * 512:(half + 1) * 512], c == 0, c == 1,
                           r=[wpr, RpeT[j]], w=[RB[2 + half]])
                OP("act", lambda e: e.activation(out=tmpD[1], in_=bank(0, 2), func=AF.Tanh, scale=0.5),
                   w=[RB[0], RB[1], Rtmp[1]])
                OP("dve", lambda e: e.scalar_tensor_tensor(out=tmpD[2], in0=tmpD[1], scalar=1.0, in1=bank(2, 2),
                                                           op0=ALU.add, op1=ALU.mult),
                   r=[Rtmp[1]], w=[RB[2], RB[3], Rtmp[2]])
                OP("dve", lambda e, ti=ti: e.scalar_tensor_tensor(out=tmpD[2], in0=tmpD[2], scalar=0.5,
                                                                 in1=hblk[:, ti, :], op0=ALU.mult, op1=ALU.add),
                   r=[Rtmp[2], Rh[ti]], w=[Rtmp[2]])
                if i < 16:
                    DMA("sp", y_p[i * 128:(i + 1) * 128, :], tmpD[2], ysl, r=[Rtmp[2]], is_output=True)
                else:
                    for bl in range(4):
                        DMA("sp", y_s[bl * 4:bl * 4 + 4, :], tmpD[2][bl * 32:bl * 32 + 4, :], ysl, r=[Rtmp[2]],
                            is_output=True)
            assert not issued and not pending, (len(issued), len(pending))
            if os.environ.get("K_BLKBAR", "0") == "1":
                S.barrier()
            if stop_after == "d4":
                S.barrier()
                dsl2 = S.slot()
                S.dma("sp", lambda e: e.dma_start(out=dbg_f[:, 0:1024], in_=tmpD[1]), dsl2, is_output=True)
                S.dma("sp", lambda e: e.dma_start(out=dbg_f[:, 1024:2048], in_=tmpD[2]), dsl2, is_output=True)
                S.dma("sp", lambda e: e.dma_start(out=dbg_f[:, 2048:3072], in_=hblk[:, 3, :]), dsl2, is_output=True)
                return finish()

        return finish()


_CACHE = {}


def _in_maps(inputs, cores):
    f = lambda a: np.ascontiguousarray(np.asarray(a))
    ck = f(inputs["cache_k"]).reshape(-1, 512)[:NPOOL * 128]
    cv = f(inputs["cache_v"]).reshape(-1, 512)[:NPOOL * 128]
    shared = dict(
        cache_k=ck, cache_v=cv,
        w_in=f(inputs["w_in"][0]), w_out=f(inputs["w_out"][0]),
        lamq1=f(inputs["lambda_q1"]), lamk1=f(inputs["lambda_k1"]),
        lamq2=f(inputs["lambda_q2"]), lamk2=f(inputs["lambda_k2"]),
        g_subln=f(inputs["g_subln"]), w_dw=f(inputs["w_dw"][0]), b_dw=f(inputs["b_dw"]),
        ln_g=f(inputs["ln_conv_g"]), ln_b=f(inputs["ln_conv_b"]),
        g_pre_mix=f(inputs["g_pre_mix"]), g_post_mix=f(inputs["g_post_mix"]),
        g_pre_ffn=f(inputs["g_pre_ffn"]), g_post_ffn=f(inputs["g_post_ffn"]),
        w_ff1=f(inputs["w_ff1"][0]), w_ff2=f(inputs["w_ff2"][0]),
        w_ple=f(inputs["w_ple"][0]), w_gate=f(inputs["w_ple_gate"][0]),
    )
    in_maps = []
    for c in cores:
        m = dict(shared)
        m["x_p"] = f(inputs["x_prompt"][c])
        m["x_s"] = f(inputs["x_sample"][4 * c:4 * c + 4]).reshape(16, D)
        m["st_conv"] = f(inputs["state_conv"][0, 4 * c:4 * c + 4]).reshape(120, 512)
        m["ptab"] = f(inputs["page_table"][4 * c:4 * c + 4]).reshape(1, 256).astype(np.int32)
        m["p_p"] = f(inputs["p_prompt"][0, c])
        m["p_s"] = f(inputs["p_sample"][0, 4 * c:4 * c + 4]).reshape(16, 256)
        in_maps.append(m)
    return in_maps


def kernel(**inputs):
    n = 8
    if "nc" not in _CACHE:
        _CACHE["nc"] = build_program()
    nc = _CACHE["nc"]
    in_maps = _in_maps(inputs, list(range(n)))
    res = run_bass_kernel_spmd(nc, in_maps, core_ids=list(range(n)))
    rs = res.results
    y_prompt = np.stack([rs[c]["y_p"] for c in range(n)])
    y_sample = np.concatenate([rs[c]["y_s"].reshape(4, 4, D) for c in range(n)])
    k_prompt = np.stack([rs[c]["k_p"].reshape(T, 4, 2, 64) for c in range(n)])[None]
    v_prompt = np.stack([rs[c]["v_p"].reshape(T, 4, 128) for c in range(n)])[None]
    conv_prompt = np.stack([rs[c]["c_p"] for c in range(n)])[None]
    k_sample = np.concatenate([rs[c]["k_s"].reshape(4, 4, 4, 2, 64) for c in range(n)])[None]
    v_sample = np.concatenate([rs[c]["v_s"].reshape(4, 4, 4, 128) for c in range(n)])[None]
    conv_sample = np.concatenate([rs[c]["c_s"].reshape(4, 30, 512) for c in range(n)])[None]
    return (y_prompt, y_sample, k_prompt, v_prompt, conv_prompt, k_sample, v_sample, conv_sample)
```
